# Optimizing a Trainium2 kernel written in Bass

```python
import math
import jax, jax.numpy as jnp
from jax import lax
import numpy as np

D_MODEL = 1024
BATCH = 8
SEQ = 4096
DEPTH = 2

HEAD_DIM = 128
N_Q_HEADS = 8
N_KV_HEADS = 2
GROUP = N_Q_HEADS // N_KV_HEADS
ATTN_WIDTH = N_Q_HEADS * HEAD_DIM
KV_WIDTH = N_KV_HEADS * HEAD_DIM
Q_BLOCK = 128
ROPE_THETA = 10000.0
ROPE_AXIS_DIM = HEAD_DIM // 2
GRID_W = 64
LRU_WIDTH = D_MODEL
LRU_BLOCKS = 8
LRU_BLOCK_W = LRU_WIDTH // LRU_BLOCKS
LRU_C = 8.0
CONV_WIDTH = 4
CONV_PAD_LEFT = 2
N_DIR = 2
D_FF = 2816
FFN_RES = 0.5
N_MOD = 9
EPS = 1e-6
SPLITS = [ATTN_WIDTH,
          ATTN_WIDTH + KV_WIDTH,
          ATTN_WIDTH + 2 * KV_WIDTH,
          ATTN_WIDTH + 2 * KV_WIDTH + LRU_WIDTH,
          ATTN_WIDTH + 2 * KV_WIDTH + 2 * LRU_WIDTH]
IN_COLS = ATTN_WIDTH + 2 * KV_WIDTH + 2 * LRU_WIDTH + 2 * D_MODEL

kernel_name = "hybrid_rglru_axial_gqa_macaron_encoder"


def _rmsnorm(x, g):
    xf = x.astype(jnp.float32)
    y = xf * lax.rsqrt(jnp.mean(xf * xf, axis=-1, keepdims=True) + EPS) * g.astype(jnp.float32)
    return y.astype(x.dtype)


def _modulate(h, shift, scale):
    return h * (1 + scale[:, None, :]) + shift[:, None, :]


def _swiglu(h, w_up, w_down):
    gate, up = jnp.split(h @ w_up, 2, axis=-1)
    return (jax.nn.silu(gate) * up) @ w_down


def _axial_rope_tables(S):
    rows = S // GRID_W
    row_ids = jnp.broadcast_to(jnp.arange(rows, dtype=jnp.float32)[:, None], (rows, GRID_W)).reshape(S)
    col_ids = jnp.broadcast_to(jnp.arange(GRID_W, dtype=jnp.float32)[None, :], (rows, GRID_W)).reshape(S)
    inv_freq = ROPE_THETA ** (-jnp.arange(0, ROPE_AXIS_DIM, 2, dtype=jnp.float32) / ROPE_AXIS_DIM)
    ang = jnp.concatenate([row_ids[:, None] * inv_freq, col_ids[:, None] * inv_freq], axis=-1)
    return jnp.cos(ang), jnp.sin(ang)


def _apply_rope(x, cos, sin):
    B, S, H, Dh = x.shape
    xp = x.astype(jnp.float32).reshape(B, S, H, Dh // 2, 2)
    x0, x1 = xp[..., 0], xp[..., 1]
    c = cos[None, :, None, :]
    s = sin[None, :, None, :]
    out = jnp.stack([x0 * c - x1 * s, x0 * s + x1 * c], axis=-1).reshape(B, S, H, Dh)
    return out.astype(x.dtype)


def _grid_attention(q, k, v, q_g, k_g):
    B, S = q.shape[0], q.shape[1]
    cos, sin = _axial_rope_tables(S)
    q = _apply_rope(_rmsnorm(q, q_g), cos, sin)
    k = _apply_rope(_rmsnorm(k, k_g), cos, sin)
    n_blk = S // Q_BLOCK
    qb = q.reshape(B, n_blk, Q_BLOCK, N_KV_HEADS, GROUP, HEAD_DIM).transpose(1, 0, 2, 3, 4, 5)
    scale = HEAD_DIM ** -0.5

    def one_block(qblk):
        s = jnp.einsum('bqkgd,bskd->bkgqs', qblk, k).astype(jnp.float32) * scale
        p = jax.nn.softmax(s, axis=-1).astype(v.dtype)
        return jnp.einsum('bkgqs,bskd->bqkgd', p, v)

    o = lax.map(one_block, qb)
    return o.transpose(1, 0, 2, 3, 4, 5).reshape(B, S, ATTN_WIDTH)


def _centred_dwconv(x, w, b):
    rhs = w[:, None, :].astype(x.dtype)
    y = lax.conv_general_dilated(x, rhs, window_strides=(1,),
                                 padding=[(CONV_PAD_LEFT, CONV_WIDTH - 1 - CONV_PAD_LEFT)],
                                 dimension_numbers=('NWC', 'WIO', 'NWC'),
                                 feature_group_count=LRU_WIDTH)
    return y + b


def _rg_lru(x, w_a, b_a, w_x, b_x, lam, reverse):
    B, S, W = x.shape
    xb = x.reshape(B, S, LRU_BLOCKS, LRU_BLOCK_W)
    r = jax.nn.sigmoid(jnp.einsum('bshi,hij->bshj', xb, w_a.astype(jnp.float32)) + b_a).reshape(B, S, W)
    i = jax.nn.sigmoid(jnp.einsum('bshi,hij->bshj', xb, w_x.astype(jnp.float32)) + b_x).reshape(B, S, W)
    log_a = -LRU_C * r * jax.nn.softplus(-lam.astype(jnp.float32))
    a = jnp.exp(log_a)
    u = jnp.sqrt(-jnp.expm1(2.0 * log_a)) * (i * x)

    def combine(e1, e2):
        a1, b1 = e1
        a2, b2 = e2
        return a1 * a2, a2 * b1 + b2

    _, h = lax.associative_scan(combine, (a, u), axis=1, reverse=reverse)
    return h


def _token_mixers(h, w_in, q_g, k_g, conv_w, conv_b, wa, ba, wx, bx, lam, w_attn_o, w_lru_o, w_out):
    B, S, _ = h.shape
    proj = h @ w_in
    q, k, v, lx, lg, gates = jnp.split(proj, SPLITS, axis=-1)
    attn = _grid_attention(q.reshape(B, S, N_Q_HEADS, HEAD_DIM),
                           k.reshape(B, S, N_KV_HEADS, HEAD_DIM),
                           v.reshape(B, S, N_KV_HEADS, HEAD_DIM), q_g, k_g)
    xc = _centred_dwconv(lx, conv_w, conv_b).astype(jnp.float32)
    h_lru = (_rg_lru(xc, wa[0], ba[0], wx[0], bx[0], lam[0], False)
             + _rg_lru(xc, wa[1], ba[1], wx[1], bx[1], lam[1], True))
    lru = h_lru.astype(h.dtype) * jax.nn.gelu(lg)
    g_attn, g_lru = jnp.split(jax.nn.sigmoid(gates), 2, axis=-1)
    merged = g_attn * (attn @ w_attn_o) + g_lru * (lru @ w_lru_o)
    return merged @ w_out


def setup_inputs(seed: int = 0) -> dict:
    key = jax.random.key(seed)
    ks = jax.random.split(key, 24)
    f32 = jnp.float32

    def nrm(k, shape, scale):
        return jax.random.normal(k, shape, f32) * scale

    a0 = jax.random.uniform(ks[17], (DEPTH, N_DIR, LRU_WIDTH), f32, 0.9, 0.999)
    return {
        "x": nrm(ks[0], (BATCH, SEQ, D_MODEL), 1.0),
        "c": nrm(ks[1], (BATCH, D_MODEL), 1.0),
        "ada_w": nrm(ks[2], (DEPTH, D_MODEL, N_MOD * D_MODEL), 0.5 * D_MODEL ** -0.5),
        "ada_b": nrm(ks[3], (DEPTH, N_MOD * D_MODEL), 0.02),
        "norm_g": 1.0 + nrm(ks[4], (DEPTH, 3, D_MODEL), 0.05),
        "ffn1_up": nrm(ks[5], (DEPTH, D_MODEL, 2 * D_FF), D_MODEL ** -0.5),
        "ffn1_down": nrm(ks[6], (DEPTH, D_FF, D_MODEL), D_FF ** -0.5),
        "w_in": nrm(ks[7], (DEPTH, D_MODEL, IN_COLS), D_MODEL ** -0.5),
        "q_norm_g": 1.0 + nrm(ks[8], (DEPTH, HEAD_DIM), 0.05),
        "k_norm_g": 1.0 + nrm(ks[9], (DEPTH, HEAD_DIM), 0.05),
        "conv_w": nrm(ks[10], (DEPTH, CONV_WIDTH, LRU_WIDTH), CONV_WIDTH ** -0.5),
        "conv_b": nrm(ks[11], (DEPTH, LRU_WIDTH), 0.02),
        "lru_wa": nrm(ks[12], (DEPTH, N_DIR, LRU_BLOCKS, LRU_BLOCK_W, LRU_BLOCK_W), LRU_BLOCK_W ** -0.5),
        "lru_ba": nrm(ks[13], (DEPTH, N_DIR, LRU_BLOCKS, LRU_BLOCK_W), 0.1),
        "lru_wx": nrm(ks[14], (DEPTH, N_DIR, LRU_BLOCKS, LRU_BLOCK_W, LRU_BLOCK_W), LRU_BLOCK_W ** -0.5),
        "lru_bx": nrm(ks[15], (DEPTH, N_DIR, LRU_BLOCKS, LRU_BLOCK_W), 0.1),
        "lru_lambda": jnp.log(a0) - jnp.log1p(-a0),
        "w_attn_o": nrm(ks[16], (DEPTH, ATTN_WIDTH, D_MODEL), ATTN_WIDTH ** -0.5),
        "w_lru_o": nrm(ks[18], (DEPTH, LRU_WIDTH, D_MODEL), LRU_WIDTH ** -0.5),
        "w_out": nrm(ks[19], (DEPTH, D_MODEL, D_MODEL), D_MODEL ** -0.5),
        "ffn2_up": nrm(ks[20], (DEPTH, D_MODEL, 2 * D_FF), D_MODEL ** -0.5),
        "ffn2_down": nrm(ks[21], (DEPTH, D_FF, D_MODEL), D_FF ** -0.5),
        "final_g": 1.0 + nrm(ks[22], (D_MODEL,), 0.05),
    }


def reference(x, c, ada_w, ada_b, norm_g, ffn1_up, ffn1_down, w_in, q_norm_g, k_norm_g,
              conv_w, conv_b, lru_wa, lru_ba, lru_wx, lru_bx, lru_lambda,
              w_attn_o, w_lru_o, w_out, ffn2_up, ffn2_down, final_g):
    B = x.shape[0]
    c_act = jax.nn.silu(c)
    for l in range(DEPTH):
        mod = (c_act @ ada_w[l] + ada_b[l]).reshape(B, N_MOD, D_MODEL)
        h = _modulate(_rmsnorm(x, norm_g[l, 0]), mod[:, 0], mod[:, 1])
        x = x + FFN_RES * mod[:, 2][:, None, :] * _swiglu(h, ffn1_up[l], ffn1_down[l])
        h = _modulate(_rmsnorm(x, norm_g[l, 1]), mod[:, 3], mod[:, 4])
        y = _token_mixers(h, w_in[l], q_norm_g[l], k_norm_g[l], conv_w[l], conv_b[l],
                          lru_wa[l], lru_ba[l], lru_wx[l], lru_bx[l], lru_lambda[l],
                          w_attn_o[l], w_lru_o[l], w_out[l])
        x = x + mod[:, 5][:, None, :] * y
        h = _modulate(_rmsnorm(x, norm_g[l, 2]), mod[:, 6], mod[:, 7])
        x = x + FFN_RES * mod[:, 8][:, None, :] * _swiglu(h, ffn2_up[l], ffn2_down[l])
    return _rmsnorm(x, final_g)
```

```python
import contextlib
import numpy as np
import concourse.bass as bass
import concourse.mybir as mybir
from concourse.bass_utils import run_bass_kernel_spmd

F32 = mybir.dt.float32
BF16 = mybir.dt.bfloat16
AF = mybir.ActivationFunctionType
ALU = mybir.AluOpType

S = 4096
D = 1024
T = 512
NT = S // T
KC = D // 128
FF = 2816
FC = FF // 128
NH = 8
NKV = 2
EPS = 1e-6
ARENA = 208896

ENGS = ("pe", "act", "dve", "pool", "sp")
SIG_ROTATE = 6000
DMA_POOL = {"sp": 24, "pool": 12, "act": 8}


class Res:
    def __init__(self, name=""):
        self.name = name
        self.last_w = None
        self.rd = {}
        self.rd_dma = []
        self.pending = []
        self.dead = False


class Op:
    __slots__ = ("eng", "fn", "deps", "dma", "sig", "sem", "val", "prev_same_sem")

    def __init__(self, eng, fn, dma):
        self.eng = eng
        self.fn = fn
        self.dma = dma
        self.deps = []
        self.sig = False
        self.sem = None
        self.val = 0
        self.prev_same_sem = None


class Prog:
    def __init__(self, nc):
        self.nc = nc
        self.ops = {e: [] for e in ENGS}
        self.live = []
        self.arena = None

    def buf(self, off, dt, dims, name=""):
        esz = 2 if dt == BF16 else 4
        n = 1
        for d in dims:
            n *= d
        nbytes = n * esz
        assert off % 4 == 0 and nbytes % 4 == 0 and off + nbytes <= ARENA, (name, off, nbytes)
        r = Res(name)
        end = off + nbytes
        keep = []
        for (o, e, old) in self.live:
            if o < end and off < e:
                if old.last_w is not None:
                    r.pending.append(old.last_w)
                r.pending.extend(old.rd.values())
                r.pending.extend(old.rd_dma)
                r.pending.extend(old.pending)
                old.dead = True
            else:
                keep.append((o, e, old))
        keep.append((off, end, r))
        self.live = keep
        v = self.arena[:, off // 4:(off + nbytes) // 4]
        if dt == BF16:
            v = v.bitcast(BF16)
        if len(dims) == 2:
            v = v.rearrange("p (a b) -> p a b", b=dims[1])
        elif len(dims) == 3:
            v = v.rearrange("p (a b c) -> p a b c", b=dims[1], c=dims[2])
        return v, r

    def add(self, eng, fn, reads=(), writes=(), dma=False):
        op = Op(eng, fn, dma)
        deps = []
        for r in reads:
            assert not r.dead, r.name
            if r.last_w is not None:
                deps.append(r.last_w)
            deps.extend(r.pending)
        for w in writes:
            assert not w.dead, w.name
            if w.last_w is not None:
                deps.append(w.last_w)
            deps.extend(w.pending)
            deps.extend(w.rd.values())
            deps.extend(w.rd_dma)
        seen = set()
        for d in deps:
            if id(d) in seen or d is op:
                continue
            seen.add(id(d))
            if d.eng == "pe" and eng == "pe" and not d.dma and not dma:
                continue
            op.deps.append(d)
        for r in reads:
            if dma:
                r.rd_dma.append(op)
            else:
                r.rd[eng] = op
        for w in writes:
            w.last_w = op
            w.rd = {}
            w.rd_dma = []
            w.pending = []
        self.ops[eng].append(op)
        return op

    def dma(self, q, out, in_, reads=(), writes=()):
        return self.add(q, lambda e: e.dma_start(out=out, in_=in_), reads, writes, dma=True)

    def emit(self, final_ops=()):
        nc = self.nc
        for e in ENGS:
            for op in self.ops[e]:
                for d in op.deps:
                    d.sig = True
        for op in final_ops:
            op.sig = True
        with contextlib.ExitStack() as st:
            sem_cnt = 0

            def new_sem(tag):
                nonlocal sem_cnt
                sem_cnt += 1
                return st.enter_context(nc.semaphore(f"s_{tag}_{sem_cnt}"))

            for e in ENGS:
                cur = None
                cnt = 0
                pool = []
                pool_last = []
                pi = 0
                for op in self.ops[e]:
                    if op.dma:
                        if len(pool) < DMA_POOL[e]:
                            pool.append(new_sem("d" + e))
                            pool_last.append(None)
                        k = pi % DMA_POOL[e]
                        pi += 1
                        op.sem = pool[k]
                        prev = pool_last[k]
                        op.prev_same_sem = prev
                        op.val = (prev.val if prev is not None else 0) + 16
                        pool_last[k] = op
                    elif op.sig:
                        if cur is None or cnt >= SIG_ROTATE:
                            cur = new_sem(e)
                            cnt = 0
                        cnt += 1
                        op.sem = cur
                        op.val = cnt
            self.n_sems = sem_cnt
            final = list(final_ops)
            with nc.Block() as block:
                def run(ename, eng):
                    known = {}
                    for op in self.ops[ename]:
                        waits = {}
                        dl = list(op.deps)
                        if op.dma and op.prev_same_sem is not None:
                            dl.append(op.prev_same_sem)
                        for d in dl:
                            k = id(d.sem)
                            if known.get(k, 0) >= d.val:
                                continue
                            if k not in waits or waits[k][1] < d.val:
                                waits[k] = (d.sem, d.val)
                        for k, (s, v) in waits.items():
                            eng.wait_ge(s, v)
                            known[k] = v
                        ins = op.fn(eng)
                        if op.dma:
                            ins.then_inc(op.sem, 16)
                        elif op.sig:
                            ins.then_inc(op.sem, 1)
                    if ename == "sp":
                        for op in final:
                            eng.wait_ge(op.sem, op.val)

                @block.tensor
                def _(eng):
                    run("pe", eng)

                @block.scalar
                def _(eng):
                    run("act", eng)

                @block.vector
                def _(eng):
                    run("dve", eng)

                @block.gpsimd
                def _(eng):
                    run("pool", eng)

                @block.sync
                def _(eng):
                    run("sp", eng)


def _vec_cols():
    cols = {}
    cur = 0

    def reg(name, n):
        nonlocal cur
        cols[name] = cur
        cur += n

    reg("c", 8)
    for l in range(2):
        reg(f"adab{l}", 72)
        reg(f"ng{l}", 24)
        reg(f"qg{l}", 1)
        reg(f"kg{l}", 1)
        reg(f"cw{l}", 32)
        reg(f"cb{l}", 8)
        reg(f"ba{l}", 16)
        reg(f"bx{l}", 16)
        reg(f"lam{l}", 16)
    reg("fg", 8)
    return cols, cur


VC, NV = _vec_cols()
STATS = {}
ROPE_PERM = np.concatenate([np.arange(0, 128, 2), np.arange(1, 128, 2)])


def _pack_vec(inp, b):
    v = np.zeros((128, NV), np.float32)

    def put(name, arr):
        arr = np.asarray(arr, np.float32)
        v[:, VC[name]:VC[name] + arr.shape[1]] = arr

    put("c", inp["c"][b].reshape(8, 128).T)
    for l in range(2):
        put(f"adab{l}", inp["ada_b"][l].reshape(9, 8, 128).transpose(2, 0, 1).reshape(128, 72))
        put(f"ng{l}", inp["norm_g"][l].reshape(3, 8, 128).transpose(2, 0, 1).reshape(128, 24))
        put(f"qg{l}", inp["q_norm_g"][l][ROPE_PERM].reshape(128, 1))
        put(f"kg{l}", inp["k_norm_g"][l][ROPE_PERM].reshape(128, 1))
        put(f"cw{l}", inp["conv_w"][l].reshape(4, 8, 128).transpose(2, 0, 1).reshape(128, 32))
        put(f"cb{l}", inp["conv_b"][l].reshape(8, 128).T)
        put(f"ba{l}", inp["lru_ba"][l].transpose(2, 0, 1).reshape(128, 16))
        put(f"bx{l}", inp["lru_bx"][l].transpose(2, 0, 1).reshape(128, 16))
        put(f"lam{l}", inp["lru_lambda"][l].reshape(2, 8, 128).transpose(2, 0, 1).reshape(128, 16))
    put("fg", inp["final_g"].reshape(8, 128).T)
    return v


def _rope_tables():
    rows = S // 64
    row_ids = np.broadcast_to(np.arange(rows, dtype=np.float32)[:, None], (rows, 64)).reshape(S)
    col_ids = np.broadcast_to(np.arange(64, dtype=np.float32)[None, :], (rows, 64)).reshape(S)
    inv_freq = (np.float32(10000.0) ** (-np.arange(0, 64, 2, dtype=np.float32) / np.float32(64))).astype(np.float32)
    ang = np.concatenate([row_ids[:, None] * inv_freq, col_ids[:, None] * inv_freq], axis=-1).astype(np.float32)
    c = np.cos(ang).astype(np.float32).T
    s = np.sin(ang).astype(np.float32).T
    cos2 = np.concatenate([c, c], 0)
    sin2 = np.concatenate([-s, s], 0)
    return np.ascontiguousarray(cos2), np.ascontiguousarray(sin2)


def build_nc(stop_after=None, n_layers=2):
    nc = bass.Bass("TRN2", target_bir_lowering=False)

    def din(name, shape, dt=F32):
        return nc.dram_tensor(name, shape, dt, kind="ExternalInput").ap()

    xT_in = din("xT", [D, S])
    vec_in = din("vec", [128, NV])
    cos_in = din("cos2", [128, S])
    sin_in = din("sin2", [128, S])
    ada_w = din("ada_w", [2, D, 9 * D])
    ffn_up = [din("ffn1_up", [2, D, 2 * FF]), din("ffn2_up", [2, D, 2 * FF])]
    ffn_down = [din("ffn1_down", [2, FF, D]), din("ffn2_down", [2, FF, D])]
    w_in = din("w_in", [2, D, 5632])
    lru_wa = din("lru_wa", [2, 2, 8, 128, 128])
    lru_wx = din("lru_wx", [2, 2, 8, 128, 128])
    w_attn_o = din("w_attn_o", [2, D, D])
    w_lru_o = din("w_lru_o", [2, D, D])
    w_out = din("w_out", [2, D, D])
    outT = nc.dram_tensor("outT", [D, S], F32, kind="ExternalOutput").ap()

    def scr(name, shape, dt):
        return nc.dram_tensor(name, shape, dt).ap()

    xa = scr("xa", [D, S], F32)
    qT = scr("qT_s", [NH, 128, S], BF16)
    kT = scr("kT_s", [NKV, 128, S], BF16)
    Vd = scr("V_s", [S, 256], BF16)
    lxT = scr("lxT_s", [D, S], F32)
    lgT = scr("lgT_s", [D, S], F32)
    gaT = scr("gaT_s", [D, S], F32)
    glT = scr("glT_s", [D, S], F32)
    lruT = scr("lruT_s", [D, S], BF16)

    R_xa = [Res(f"xa{t}") for t in range(NT)]
    R_q = [[Res() for _ in range(NT)] for _ in range(NH)]
    R_k = [Res() for _ in range(NT)]
    R_v = [Res() for _ in range(NT)]
    R_lx = [Res() for _ in range(KC)]
    R_lg = [Res() for _ in range(KC)]
    R_ga = [[Res() for _ in range(NT)] for _ in range(KC)]
    R_gl = [[Res() for _ in range(NT)] for _ in range(KC)]
    R_lru = [Res() for _ in range(KC)]
    R_lx_t = [[Res() for _ in range(NT)] for _ in range(KC)]
    R_lg_t = [[Res() for _ in range(NT)] for _ in range(KC)]

    P = Prog(nc)
    with contextlib.ExitStack() as st:
        arena = st.enter_context(nc.sbuf_tensor("arena", [128, ARENA // 4], F32))
        P.arena = arena
        vec = st.enter_context(nc.sbuf_tensor("vec_sb", [128, NV], F32))
        R_vec = Res("vec")
        NDV = 144 + 3 * 48 + 32 + 8
        dv = st.enter_context(nc.sbuf_tensor("dv_sb", [128, NDV], F32))
        R_dv = Res("dv")
        ones = st.enter_context(nc.sbuf_tensor("ones_sb", [128, 128], BF16))
        R_ones = Res("ones")
        cact = st.enter_context(nc.sbuf_tensor("cact_sb", [128, 8], BF16))
        R_cact = Res("cact")
        PS = [st.enter_context(nc.psum_tensor(f"ps{i}", [128, 512], F32)) for i in range(8)]
        RPS = [Res(f"ps{i}") for i in range(8)]

        def vcol(name, i=0, n=1):
            return vec[:, VC[name] + i:VC[name] + i + n]

        DV_MOD, DV_A, DV_B, DV_G, DV_CL, DV_TMP = 0, 144, 192, 240, 288, 320

        def dvc(base, i, n=1):
            return dv[:, base + i:base + i + n]

        P.dma("sp", vec[:], vec_in, writes=[R_vec])
        P.add("dve", lambda e: e.memset(ones[:], 1.0), [], [R_ones])

        def mod_phase():
            P.add("act", lambda e: e.activation(out=cact[:], in_=vcol("c", 0, 8), func=AF.Silu), [R_vec], [R_cact])
            WOFF = 90112
            nbuf = 2
            wb = [P.buf(WOFF + i * 16384, BF16, [KC, 1024], f"adaw{i}") for i in range(nbuf)]
            it = 0
            for l in range(n_layers):
                for m in range(9):
                    wv, wr = wb[it % nbuf]
                    src = ada_w[l].rearrange("(k p) n -> p k n", p=128)[:, :, m * 1024:(m + 1) * 1024]
                    P.dma("pool", wv, src, writes=[wr])
                    pb = it % 2
                    for j in range(8):
                        for k in range(8):
                            P.add("pe", lambda e, wv=wv, j=j, k=k, pb=pb: e.matmul(
                                PS[pb][:, j:j + 1], lhsT=wv[:, k, j * 128:(j + 1) * 128], rhs=cact[:, k:k + 1],
                                start=(k == 0), stop=(k == 7)), [wr, R_cact], [RPS[pb]])
                    P.add("dve", lambda e, l=l, m=m, pb=pb: e.tensor_tensor(
                        out=dvc(DV_MOD, l * 72 + m * 8, 8), in0=PS[pb][:, 0:8], in1=vcol(f"adab{l}", m * 8, 8), op=ALU.add),
                        [RPS[pb], R_vec], [R_dv])
                    it += 1
            for l in range(n_layers):
                for s in range(3):
                    P.add("dve", lambda e, l=l, s=s: e.scalar_tensor_tensor(
                        out=dvc(DV_A, l * 24 + s * 8, 8), in0=dvc(DV_MOD, l * 72 + (3 * s + 1) * 8, 8), scalar=1.0,
                        in1=vcol(f"ng{l}", s * 8, 8), op0=ALU.add, op1=ALU.mult), [R_dv, R_vec], [R_dv])
                    P.add("dve", lambda e, l=l, s=s: e.tensor_copy(
                        out=dvc(DV_B, l * 24 + s * 8, 8), in_=dvc(DV_MOD, l * 72 + (3 * s) * 8, 8)), [R_dv], [R_dv])
                    P.add("dve", lambda e, l=l, s=s: e.tensor_scalar(
                        out=dvc(DV_G, l * 24 + s * 8, 8), in0=dvc(DV_MOD, l * 72 + (3 * s + 2) * 8, 8),
                        scalar1=(1.0 if s == 1 else 0.5), scalar2=0.0, op0=ALU.mult, op1=ALU.add), [R_dv], [R_dv])
                P.add("act", lambda e, l=l: e.activation(out=dvc(DV_CL, l * 16, 16), in_=vcol(f"lam{l}", 0, 16),
                                                         func=AF.Exp, scale=-1.0), [R_vec], [R_dv])
                P.add("act", lambda e, l=l: e.activation(out=dvc(DV_CL, l * 16, 16), in_=dvc(DV_CL, l * 16, 16),
                                                         func=AF.Ln, bias=1.0), [R_dv], [R_dv])
                P.add("dve", lambda e, l=l: e.tensor_scalar(out=dvc(DV_CL, l * 16, 16), in0=dvc(DV_CL, l * 16, 16),
                                                            scalar1=-8.0, scalar2=0.0, op0=ALU.mult, op1=ALU.add), [R_dv], [R_dv])

        def norm_tile(XN, r_xn, HT, r_ht, SQ, RSTD, r_rstd, a_base, b_base, ps_i, rdv=True):
            for k in range(KC):
                sq, r_sq = SQ[k % 2]
                P.add("act", lambda e, k=k, sq=sq: e.activation(out=sq, in_=XN[:, k, :], func=AF.Square), [r_xn], [r_sq])
                P.add("pe", lambda e, k=k, sq=sq: e.matmul(PS[ps_i][:], lhsT=ones[:], rhs=sq, start=(k == 0), stop=(k == 7)),
                      [R_ones, r_sq], [RPS[ps_i]])
            P.add("act", lambda e: e.activation(out=RSTD, in_=PS[ps_i][:], func=AF.Ln, scale=1.0 / D, bias=EPS), [RPS[ps_i]], [r_rstd])
            P.add("act", lambda e: e.activation(out=RSTD, in_=RSTD, func=AF.Exp, scale=-0.5), [r_rstd], [r_rstd])
            for k in range(KC):
                P.add("dve", lambda e, k=k: e.scalar_tensor_tensor(
                    out=XN[:, k, :], in0=XN[:, k, :], scalar=dvc(a_base, k) if rdv else a_base[:, k:k + 1], in1=RSTD,
                    op0=ALU.mult, op1=ALU.mult), [r_xn, r_rstd, R_dv if rdv else R_vec], [r_xn])
                if b_base is not None:
                    P.add("act", lambda e, k=k: e.activation(out=HT[:, k, :], in_=XN[:, k, :], func=AF.Identity,
                                                             bias=dvc(b_base, k), scale=1.0), [r_xn, R_dv], [r_ht])

        def xview(dram, t):
            return dram.rearrange("(k p) s -> p k s", p=128)[:, :, t * T:(t + 1) * T]

        def ffn_phase(l, which, src):
            s = 0 if which == 0 else 2
            up = ffn_up[which][l]
            down = ffn_down[which][l]
            WA = [P.buf(g * 8192, BF16, [KC, 512], f"WA{g}") for g in range(11)]
            WB = [P.buf(90112 + h * 22528, BF16, [11, 1024], f"WB{h}") for h in range(2)]
            o = 135168
            XN = [None]
            XNv, r_xn = P.buf(o, F32, [KC, T], "XN"); o += 16384
            XRv, r_xr = P.buf(o, F32, [KC, T], "XR"); o += 16384
            HTv, r_ht = P.buf(o, BF16, [KC, T], "HT"); o += 8192
            ATc = [P.buf(o + j * 1024, BF16, [T], f"ACTT{j}") for j in range(FC)]; o += 22528
            SQ = []
            for i in range(2):
                v, r = P.buf(o, BF16, [T], "SQ"); o += 1024
                SQ.append((v, r))
            RSTD, r_rstd = P.buf(o, F32, [T], "RSTD"); o += 2048
            SG = []
            for i in range(2):
                v, r = P.buf(o, F32, [T], "SG"); o += 2048
                SG.append((v, r))
            upv = up.rearrange("(k p) n -> p k n", p=128)
            for g in range(11):
                P.dma("pool", WA[g][0], upv[:, :, g * 512:(g + 1) * 512], writes=[WA[g][1]])
            dnv = down.rearrange("(c p) n -> p c n", p=128)
            for h in range(2):
                P.dma("pool", WB[h][0], dnv[:, h * 11:(h + 1) * 11, :], writes=[WB[h][1]])

            def wa_cols(c0):
                g = c0 // 512
                off = c0 % 512
                return WA[g][0], off, WA[g][1]

            for t in range(NT):
                rsrc = R_xa[t] if src is xa else Res()
                P.dma("sp", XNv, xview(src, t), reads=[rsrc], writes=[r_xn])
                norm_tile(XNv, r_xn, HTv, r_ht, SQ, RSTD, r_rstd, DV_A + l * 24 + s * 8, DV_B + l * 24 + s * 8, 6)
                P.dma("sp", XRv, xview(src, t), reads=[rsrc], writes=[r_xr])
                for j in range(FC):
                    pg, pu = (0, 1) if j % 2 == 0 else (2, 3)
                    for (pb, c0) in ((pg, j * 128), (pu, FF + j * 128)):
                        wv, off, wr = wa_cols(c0)
                        for k in range(KC):
                            P.add("pe", lambda e, wv=wv, off=off, k=k, pb=pb: e.matmul(
                                PS[pb][:], lhsT=wv[:, k, off:off + 128], rhs=HTv[:, k, :], start=(k == 0), stop=(k == 7)),
                                [wr, r_ht], [RPS[pb]])
                    sg, r_sg = SG[j % 2]
                    P.add("act", lambda e, sg=sg, pg=pg: e.activation(out=sg, in_=PS[pg][:], func=AF.Silu), [RPS[pg]], [r_sg])
                    P.add("dve", lambda e, sg=sg, pu=pu, j=j: e.tensor_tensor(out=ATc[j][0], in0=sg, in1=PS[pu][:], op=ALU.mult),
                          [r_sg, RPS[pu]], [ATc[j][1]])
                for d in range(KC):
                    pb = 4 + d % 2
                    for c in range(FC):
                        wv, wr = WB[c // 11]
                        P.add("pe", lambda e, wv=wv, c=c, d=d, pb=pb: e.matmul(
                            PS[pb][:], lhsT=wv[:, c % 11, d * 128:(d + 1) * 128], rhs=ATc[c][0], start=(c == 0), stop=(c == FC - 1)),
                            [wr, ATc[c][1]], [RPS[pb]])
                    P.add("dve", lambda e, d=d, pb=pb: e.scalar_tensor_tensor(
                        out=XRv[:, d, :], in0=PS[pb][:], scalar=dvc(DV_G + l * 24 + s * 8, d), in1=XRv[:, d, :],
                        op0=ALU.mult, op1=ALU.add), [RPS[pb], R_dv, r_xr], [r_xr])
                P.dma("sp", xview(xa, t), XRv, reads=[r_xr], writes=[R_xa[t]])

        def m1_phase(l):
            WA = [P.buf(g * 8192, BF16, [KC, 512], f"WA{g}") for g in range(11)]
            wv_in = w_in[l].rearrange("(k p) n -> p k n", p=128)
            for g in range(11):
                P.dma("pool", WA[g][0], wv_in[:, :, g * 512:(g + 1) * 512], writes=[WA[g][1]])
            o = 90112

            def alloc(dt, dims, name):
                nonlocal o
                v, r = P.buf(o, dt, dims, name)
                n = 1
                for d_ in dims:
                    n *= d_
                o += n * (2 if dt == BF16 else 4)
                return v, r

            XN = [alloc(F32, [KC, T], f"XN{i}") for i in range(2)]
            HTv, r_ht = alloc(BF16, [KC, T], "HT")
            SQ = [alloc(BF16, [T], "SQ") for i in range(2)]
            RSTD, r_rstd = alloc(F32, [T], "RSTD")
            STG = [alloc(F32, [T], f"STG{i}") for i in range(4)]
            QS = [alloc(BF16, [T], f"QS{i}") for i in range(2)]
            VS, r_vs = alloc(BF16, [4, 256], "VS")
            COS = [alloc(F32, [T], f"COS{i}") for i in range(2)]
            SIN = [alloc(F32, [T], f"SIN{i}") for i in range(2)]
            SQH = [alloc(BF16, [T], f"SQH{i}") for i in range(2)]
            RSH = [alloc(F32, [T], f"RSH{i}") for i in range(2)]
            QN = [alloc(F32, [T], f"QN{i}") for i in range(2)]
            SW = [alloc(F32, [T], f"SW{i}") for i in range(2)]
            T1 = [alloc(F32, [T], f"T1{i}") for i in range(2)]

            def wa_cols(c0):
                g = c0 // 512
                return WA[g][0], c0 % 512, WA[g][1]

            stg_i = 0
            for t in range(NT):
                XNv, r_xn = XN[t % 2]
                cosv, r_cos = COS[t % 2]
                sinv, r_sin = SIN[t % 2]
                P.dma("sp", XNv, xview(xa, t), reads=[R_xa[t]], writes=[r_xn])
                P.dma("sp", cosv, cos_in[:, t * T:(t + 1) * T], writes=[r_cos])
                P.dma("sp", sinv, sin_in[:, t * T:(t + 1) * T], writes=[r_sin])
                norm_tile(XNv, r_xn, HTv, r_ht, SQ, RSTD, r_rstd, DV_A + l * 24 + 8, DV_B + l * 24 + 8, 6)
                pending = None
                hi = 0
                for j in range(44):
                    if j in (10, 11):
                        continue
                    pb = j % 4
                    wv, off, wr = wa_cols(j * 128)
                    for k in range(KC):
                        P.add("pe", lambda e, wv=wv, off=off, k=k, pb=pb: e.matmul(
                            PS[pb][:], lhsT=wv[:, k, off:off + 128], rhs=HTv[:, k, :], start=(k == 0), stop=(k == 7)),
                            [wr, r_ht], [RPS[pb]])
                    if pending is not None:
                        pending()
                        pending = None
                    if j < 10:
                        i2 = hi % 2
                        hi += 1
                        sqh, r_sqh = SQH[i2]
                        rsh, r_rsh = RSH[i2]
                        qn, r_qn = QN[i2]
                        sw, r_sw = SW[i2]
                        t1, r_t1 = T1[i2]
                        qs, r_qs = QS[i2]
                        gname = f"qg{l}" if j < 8 else f"kg{l}"
                        P.add("act", lambda e, sqh=sqh, pb=pb: e.activation(out=sqh, in_=PS[pb][:], func=AF.Square), [RPS[pb]], [r_sqh])

                        def fin(j=j, pb=pb, sqh=sqh, r_sqh=r_sqh, rsh=rsh, r_rsh=r_rsh, qn=qn, r_qn=r_qn, sw=sw, r_sw=r_sw,
                                t1=t1, r_t1=r_t1, qs=qs, r_qs=r_qs, gname=gname, t=t,
                                cosv=cosv, r_cos=r_cos, sinv=sinv, r_sin=r_sin):
                            P.add("pe", lambda e: e.matmul(PS[7][:], lhsT=ones[:], rhs=sqh, start=True, stop=True), [R_ones, r_sqh], [RPS[7]])
                            P.add("act", lambda e: e.activation(out=rsh, in_=PS[7][:], func=AF.Ln, scale=1.0 / 128, bias=EPS), [RPS[7]], [r_rsh])
                            P.add("act", lambda e: e.activation(out=rsh, in_=rsh, func=AF.Exp, scale=-0.5), [r_rsh], [r_rsh])
                            P.add("dve", lambda e: e.scalar_tensor_tensor(out=qn, in0=PS[pb][:], scalar=vcol(gname), in1=rsh,
                                                                          op0=ALU.mult, op1=ALU.mult), [RPS[pb], R_vec, r_rsh], [r_qn])
                            P.add("pool", lambda e: e.tensor_copy(out=sw[0:64, :], in_=qn[64:128, :]), [r_qn], [r_sw])
                            P.add("pool", lambda e: e.tensor_copy(out=sw[64:128, :], in_=qn[0:64, :]), [r_qn], [r_sw])
                            P.add("dve", lambda e: e.tensor_tensor(out=t1, in0=qn, in1=cosv, op=ALU.mult), [r_qn, r_cos], [r_t1])
                            P.add("dve", lambda e: e.tensor_tensor(out=sw, in0=sw, in1=sinv, op=ALU.mult), [r_sw, r_sin], [r_sw])
                            P.add("dve", lambda e: e.tensor_tensor(out=qs, in0=t1, in1=sw, op=ALU.add), [r_t1, r_sw], [r_qs])
                            if j < 8:
                                P.dma("sp", qT[j][:, t * T:(t + 1) * T], qs, reads=[r_qs], writes=[R_q[j][t]])
                            else:
                                P.dma("sp", kT[j - 8][:, t * T:(t + 1) * T], qs, reads=[r_qs], writes=[R_k[t]])
                        pending = fin
                    else:
                        stg, r_stg = STG[stg_i % 4]
                        stg_i += 1
                        if j < 20:
                            kk = j - 12
                            P.add("act", lambda e, stg=stg, pb=pb: e.activation(out=stg, in_=PS[pb][:], func=AF.Copy), [RPS[pb]], [r_stg])
                            P.dma("sp", lxT[kk * 128:(kk + 1) * 128, t * T:(t + 1) * T], stg, reads=[r_stg], writes=[R_lx_t[kk][t]])
                        elif j < 28:
                            kk = j - 20
                            P.add("act", lambda e, stg=stg, pb=pb: e.activation(out=stg, in_=PS[pb][:], func=AF.Gelu), [RPS[pb]], [r_stg])
                            P.dma("sp", lgT[kk * 128:(kk + 1) * 128, t * T:(t + 1) * T], stg, reads=[r_stg], writes=[R_lg_t[kk][t]])
                        else:
                            kk = (j - 28) % 8
                            dst, rr = (gaT, R_ga) if j < 36 else (glT, R_gl)
                            P.add("act", lambda e, stg=stg, pb=pb: e.activation(out=stg, in_=PS[pb][:], func=AF.Sigmoid), [RPS[pb]], [r_stg])
                            P.dma("sp", dst[kk * 128:(kk + 1) * 128, t * T:(t + 1) * T], stg, reads=[r_stg], writes=[rr[kk][t]])
                if pending is not None:
                    pending()
                for blk in range(4):
                    pb = 4 + blk % 2
                    wv, off, wr = wa_cols(1280)
                    for k in range(KC):
                        P.add("pe", lambda e, wv=wv, off=off, k=k, pb=pb, blk=blk: e.matmul(
                            PS[pb][:, 0:256], lhsT=HTv[:, k, blk * 128:(blk + 1) * 128], rhs=wv[:, k, off:off + 256],
                            start=(k == 0), stop=(k == 7)), [wr, r_ht], [RPS[pb]])
                    P.add("dve", lambda e, pb=pb, blk=blk: e.tensor_copy(out=VS[:, blk, :], in_=PS[pb][:, 0:256]), [RPS[pb]], [r_vs])
                P.dma("sp", Vd.rearrange("(b p) c -> p b c", p=128)[:, t * 4:(t + 1) * 4, :], VS, reads=[r_vs], writes=[R_v[t]])

        def m2_phase(l):
            o = 90112

            def alloc(dt, dims, name):
                nonlocal o
                v, r = P.buf(o, dt, dims, name)
                n = 1
                for d_ in dims:
                    n *= d_
                o += n * (2 if dt == BF16 else 4)
                return v, r

            WG, r_wg = alloc(BF16, [32, 128], "WG")
            LXP, r_lxp = alloc(F32, [S + 4], "LXP")
            XC, r_xc = alloc(F32, [S], "XC")
            XCB, r_xcb = alloc(BF16, [S], "XCB")
            Av, r_a = alloc(F32, [S], "A")
            Uv, r_u = alloc(F32, [S], "U")
            S0 = alloc(F32, [S], "S0")
            S1 = alloc(F32, [S], "S1")
            P.dma("pool", WG[:, 0:16, :], lru_wa[l].rearrange("d b i j -> i (d b) j"), writes=[r_wg])
            P.dma("pool", WG[:, 16:32, :], lru_wx[l].rearrange("d b i j -> i (d b) j"), writes=[r_wg])
            P.add("dve", lambda e: e.memset(LXP[:, 0:2], 0.0), [], [r_lxp])
            P.add("dve", lambda e: e.memset(LXP[:, S + 2:S + 4], 0.0), [r_lxp], [r_lxp])
            for b in range(KC):
                P.dma("sp", LXP[:, 2:S + 2], lxT[b * 128:(b + 1) * 128, :], reads=[R_lx_t[b][t] for t in range(NT)] + [r_lxp], writes=[r_lxp])
                P.add("dve", lambda e, b=b: e.tensor_scalar(out=XC, in0=LXP[:, 0:S], scalar1=vcol(f"cw{l}", 0 * 8 + b),
                                                            scalar2=vcol(f"cb{l}", b), op0=ALU.mult, op1=ALU.add), [r_lxp, R_vec], [r_xc])
                for j in range(1, 4):
                    P.add("dve", lambda e, b=b, j=j: e.scalar_tensor_tensor(out=XC, in0=LXP[:, j:j + S], scalar=vcol(f"cw{l}", j * 8 + b),
                                                                             in1=XC, op0=ALU.mult, op1=ALU.add), [r_lxp, R_vec, r_xc], [r_xc])
                P.add("act", lambda e: e.activation(out=XCB, in_=XC, func=AF.Copy), [r_xc], [r_xcb])
                for dr in range(2):
                    Hv, r_h = (S0, S1)[dr]
                    for t in range(NT):
                        pa, px = (0, 1) if t % 2 == 0 else (2, 3)
                        sl = slice(t * T, (t + 1) * T)
                        P.add("pe", lambda e, pa=pa, sl=sl, dr=dr, b=b: e.matmul(PS[pa][:], lhsT=WG[:, dr * 8 + b, :], rhs=XCB[:, sl],
                                                                               start=True, stop=True), [r_wg, r_xcb], [RPS[pa]])
                        P.add("pe", lambda e, px=px, sl=sl, dr=dr, b=b: e.matmul(PS[px][:], lhsT=WG[:, 16 + dr * 8 + b, :], rhs=XCB[:, sl],
                                                                               start=True, stop=True), [r_wg, r_xcb], [RPS[px]])
                        P.add("act", lambda e, pa=pa, sl=sl, dr=dr, b=b: e.activation(out=Av[:, sl], in_=PS[pa][:], func=AF.Sigmoid,
                                                                                     bias=vcol(f"ba{l}", dr * 8 + b)), [RPS[pa], R_vec], [r_a])
                        P.add("act", lambda e, px=px, sl=sl, dr=dr, b=b: e.activation(out=Uv[:, sl], in_=PS[px][:], func=AF.Sigmoid,
                                                                                     bias=vcol(f"bx{l}", dr * 8 + b)), [RPS[px], R_vec], [r_u])
                    P.add("act", lambda e, dr=dr, b=b: e.activation(out=Av, in_=Av, func=AF.Exp, scale=dvc(DV_CL, l * 16 + dr * 8 + b)),
                          [r_a, R_dv], [r_a])
                    P.add("act", lambda e, Hv=Hv: e.activation(out=Hv, in_=Av, func=AF.Square), [r_a], [r_h])
                    P.add("act", lambda e, Hv=Hv: e.activation(out=Hv, in_=Hv, func=AF.Sqrt, scale=-1.0, bias=1.0), [r_h], [r_h])
                    P.add("dve", lambda e: e.tensor_tensor(out=Uv, in0=Uv, in1=XC, op=ALU.mult), [r_u, r_xc], [r_u])
                    P.add("dve", lambda e, Hv=Hv: e.tensor_tensor(out=Uv, in0=Uv, in1=Hv, op=ALU.mult), [r_u, r_h], [r_u])
                    if dr == 0:
                        P.add("dve", lambda e, Hv=Hv: e.tensor_tensor_scan(out=Hv, data0=Av, data1=Uv, initial=0.0, op0=ALU.mult, op1=ALU.add),
                              [r_a, r_u], [r_h])
                    else:
                        P.add("dve", lambda e, Hv=Hv: e.tensor_tensor_scan(out=Hv[:, ::-1], data0=Av[:, ::-1], data1=Uv[:, ::-1], initial=0.0,
                                                                         op0=ALU.mult, op1=ALU.add), [r_a, r_u], [r_h])
                P.add("dve", lambda e: e.tensor_tensor(out=S0[0], in0=S0[0], in1=S1[0], op=ALU.add), [S0[1], S1[1]], [S0[1]])
                P.dma("sp", Av, lgT[b * 128:(b + 1) * 128, :], reads=[R_lg_t[b][t] for t in range(NT)], writes=[r_a])
                P.add("dve", lambda e: e.tensor_tensor(out=XCB, in0=S0[0], in1=Av, op=ALU.mult), [S0[1], r_a, r_xcb], [r_xcb])
                P.dma("sp", lruT[b * 128:(b + 1) * 128, :], XCB, reads=[r_xcb], writes=[R_lru[b]])

        def m3_phase(l):
            o = 0

            def alloc(dt, dims, name):
                nonlocal o
                v, r = P.buf(o, dt, dims, name)
                n = 1
                for d_ in dims:
                    n *= d_
                o += n * (2 if dt == BF16 else 4)
                return v, r

            WAO, r_wao = alloc(BF16, [KC, D], "WAO")
            WLO, r_wlo = alloc(BF16, [KC, D], "WLO")
            WO, r_wo = alloc(BF16, [KC, D], "WO")
            KT, r_kt = alloc(BF16, [NKV, S], "KT")
            VV, r_vv = alloc(BF16, [32, 256], "VV")
            QB = [alloc(BF16, [T], f"QB{i}") for i in range(2)]
            PT = [alloc(BF16, [T], f"PT{i}") for i in range(3)]
            ATT = [alloc(BF16, [T], f"ATT{i}") for i in range(KC)]
            LRT, r_lrt = alloc(BF16, [KC, T], "LRT")
            GA, r_ga = alloc(F32, [KC, T], "GA")
            GL, r_gl = alloc(F32, [KC, T], "GL")
            MT = [alloc(BF16, [T], f"MT{i}") for i in range(KC)]
            XR, r_xr = alloc(F32, [KC, T], "XR")
            REC, r_rec = alloc(F32, [T], "REC")
            M1 = [alloc(F32, [T], f"M1{i}") for i in range(2)]
            M2 = [alloc(F32, [T], f"M2{i}") for i in range(2)]
            P.dma("pool", WAO, w_attn_o[l].rearrange("(k p) n -> p k n", p=128), writes=[r_wao])
            P.dma("pool", WLO, w_lru_o[l].rearrange("(k p) n -> p k n", p=128), writes=[r_wlo])
            P.dma("pool", WO, w_out[l].rearrange("(k p) n -> p k n", p=128), writes=[r_wo])
            P.dma("sp", KT, kT.rearrange("g p s -> p g s"), reads=R_k, writes=[r_kt])
            P.dma("sp", VV, Vd.rearrange("(b p) c -> p b c", p=128), reads=R_v, writes=[r_vv])
            scale = 128.0 ** -0.5
            qi = 0
            for t in range(NT):
                tsl = slice(t * T, (t + 1) * T)
                P.dma("sp", LRT, lruT.rearrange("(k p) s -> p k s", p=128)[:, :, tsl], reads=R_lru, writes=[r_lrt])
                P.dma("sp", GA, gaT.rearrange("(k p) s -> p k s", p=128)[:, :, tsl], reads=[R_ga[kk][t] for kk in range(KC)], writes=[r_ga])
                P.dma("sp", GL, glT.rearrange("(k p) s -> p k s", p=128)[:, :, tsl], reads=[R_gl[kk][t] for kk in range(KC)], writes=[r_gl])
                P.dma("sp", XR, xview(xa, t), reads=[R_xa[t]], writes=[r_xr])
                for h in range(NH):
                    g = h // 4
                    qb, r_qb = QB[qi % 2]
                    po, psm = (4, 5) if qi % 2 == 0 else (6, 7)
                    qi += 1
                    P.dma("sp", qb, qT[h][:, tsl], reads=[R_q[h][t]], writes=[r_qb])

                    def qk(c, g=g, qb=qb, r_qb=r_qb):
                        pb = c % 2
                        P.add("pe", lambda e: e.matmul(PS[pb][:], lhsT=KT[:, g, c * 128:(c + 1) * 128], rhs=qb, start=True, stop=True),
                              [r_kt, r_qb], [RPS[pb]])
                        pt, r_pt = PT[c % 3]
                        P.add("act", lambda e: e.activation(out=pt, in_=PS[pb][:], func=AF.Exp, scale=scale), [RPS[pb]], [r_pt])

                    def pv(c, g=g, po=po, psm=psm):
                        pt, r_pt = PT[c % 3]
                        P.add("pe", lambda e: e.matmul(PS[po][:], lhsT=VV[:, c, g * 128:(g + 1) * 128], rhs=pt, start=(c == 0), stop=(c == 31)),
                              [r_vv, r_pt], [RPS[po]])
                        P.add("pe", lambda e: e.matmul(PS[psm][:], lhsT=ones[:], rhs=pt, start=(c == 0), stop=(c == 31)),
                              [R_ones, r_pt], [RPS[psm]])

                    qk(0)
                    for c in range(32):
                        if c + 1 < 32:
                            qk(c + 1)
                        pv(c)
                    P.add("dve", lambda e, psm=psm: e.reciprocal(out=REC, in_=PS[psm][:]), [RPS[psm]], [r_rec])
                    P.add("dve", lambda e, po=po, h=h: e.tensor_tensor(out=ATT[h][0], in0=PS[po][:], in1=REC, op=ALU.mult),
                          [RPS[po], r_rec], [ATT[h][1]])
                for d in range(KC):
                    pa, pl = (0, 1) if d % 2 == 0 else (2, 3)
                    dsl = slice(d * 128, (d + 1) * 128)
                    for k in range(KC):
                        P.add("pe", lambda e, pa=pa, k=k, dsl=dsl: e.matmul(PS[pa][:], lhsT=WAO[:, k, dsl], rhs=ATT[k][0], start=(k == 0), stop=(k == 7)),
                              [r_wao, ATT[k][1]], [RPS[pa]])
                    for k in range(KC):
                        P.add("pe", lambda e, pl=pl, k=k, dsl=dsl: e.matmul(PS[pl][:], lhsT=WLO[:, k, dsl], rhs=LRT[:, k, :], start=(k == 0), stop=(k == 7)),
                              [r_wlo, r_lrt], [RPS[pl]])
                    m1, r_m1 = M1[d % 2]
                    m2, r_m2 = M2[d % 2]
                    P.add("dve", lambda e, m1=m1, pa=pa, d=d: e.tensor_tensor(out=m1, in0=PS[pa][:], in1=GA[:, d, :], op=ALU.mult), [RPS[pa], r_ga], [r_m1])
                    P.add("dve", lambda e, m2=m2, pl=pl, d=d: e.tensor_tensor(out=m2, in0=PS[pl][:], in1=GL[:, d, :], op=ALU.mult), [RPS[pl], r_gl], [r_m2])
                    P.add("pool", lambda e, m1=m1, m2=m2, d=d: e.tensor_tensor(out=MT[d][0], in0=m1, in1=m2, op=ALU.add), [r_m1, r_m2], [MT[d][1]])
                for d in range(KC):
                    pb = d % 2
                    dsl = slice(d * 128, (d + 1) * 128)
                    for k in range(KC):
                        P.add("pe", lambda e, pb=pb, k=k, dsl=dsl: e.matmul(PS[pb][:], lhsT=WO[:, k, dsl], rhs=MT[k][0], start=(k == 0), stop=(k == 7)),
                              [r_wo, MT[k][1]], [RPS[pb]])
                    P.add("dve", lambda e, pb=pb, d=d: e.scalar_tensor_tensor(out=XR[:, d, :], in0=PS[pb][:], scalar=dvc(DV_G + l * 24 + 8, d),
                                                                             in1=XR[:, d, :], op0=ALU.mult, op1=ALU.add), [RPS[pb], R_dv, r_xr], [r_xr])
                P.dma("sp", xview(xa, t), XR, reads=[r_xr], writes=[R_xa[t]])

        def final_phase():
            o = 90112
            XN = []
            for i in range(2):
                XN.append(P.buf(o, F32, [KC, T], f"FXN{i}")); o += 16384
            SQ = []
            for i in range(2):
                SQ.append(P.buf(o, BF16, [T], "FSQ")); o += 1024
            RSTD, r_rstd = P.buf(o, F32, [T], "FRSTD"); o += 2048
            fin = []
            for t in range(NT):
                XNv, r_xn = XN[t % 2]
                P.dma("sp", XNv, xview(xa, t), reads=[R_xa[t]], writes=[r_xn])
                norm_tile(XNv, r_xn, None, None, SQ, RSTD, r_rstd, vec[:, VC["fg"]:VC["fg"] + 8], None, 6, rdv=False)
                fin.append(P.dma("sp", xview(outT, t), XNv, reads=[r_xn]))
            return fin

        def copy_out():
            fin = []
            for t in range(NT):
                fin.append(P.dma("sp", outT[:, t * T:(t + 1) * T], xa[:, t * T:(t + 1) * T], reads=[R_xa[t]]))
            return fin

        phases = []
        for l in range(n_layers):
            phases += [("ffn", l, 0), ("m1", l), ("m2", l), ("m3", l), ("ffn", l, 1)]
        mod_phase()
        fin = None
        for i, ph in enumerate(phases):
            if ph[0] == "ffn":
                ffn_phase(ph[1], ph[2], xT_in if i == 0 else xa)
            elif ph[0] == "m1":
                m1_phase(ph[1])
            elif ph[0] == "m2":
                m2_phase(ph[1])
            elif ph[0] == "m3":
                m3_phase(ph[1])
            if stop_after is not None and i == stop_after:
                fin = copy_out()
                break
        if fin is None:
            fin = final_phase()
        P.emit(fin)
        STATS.clear()
        STATS.update({e: len(P.ops[e]) for e in ENGS})
        STATS["sems"] = P.n_sems
    return nc


def make_in_maps(inputs):
    inp = {k: np.asarray(v) for k, v in inputs.items()}
    cos2, sin2 = _rope_tables()
    w_in = np.array(inp["w_in"], np.float32, copy=True)
    for h in range(NH + NKV):
        w_in[:, :, h * 128:(h + 1) * 128] = inp["w_in"][:, :, h * 128 + ROPE_PERM]
    shared = {
        "cos2": cos2, "sin2": sin2,
        "ada_w": np.ascontiguousarray(inp["ada_w"], np.float32),
        "ffn1_up": np.ascontiguousarray(inp["ffn1_up"], np.float32),
        "ffn2_up": np.ascontiguousarray(inp["ffn2_up"], np.float32),
        "ffn1_down": np.ascontiguousarray(inp["ffn1_down"], np.float32),
        "ffn2_down": np.ascontiguousarray(inp["ffn2_down"], np.float32),
        "w_in": w_in,
        "lru_wa": np.ascontiguousarray(inp["lru_wa"], np.float32),
        "lru_wx": np.ascontiguousarray(inp["lru_wx"], np.float32),
        "w_attn_o": np.ascontiguousarray(inp["w_attn_o"], np.float32),
        "w_lru_o": np.ascontiguousarray(inp["w_lru_o"], np.float32),
        "w_out": np.ascontiguousarray(inp["w_out"], np.float32),
    }
    maps = []
    for b in range(inp["x"].shape[0]):
        m = dict(shared)
        m["xT"] = np.ascontiguousarray(inp["x"][b].T, np.float32)
        m["vec"] = _pack_vec(inp, b)
        maps.append(m)
    return maps


_NC_CACHE = {}


def kernel(**inputs):
    if "nc" not in _NC_CACHE:
        _NC_CACHE["nc"] = build_nc()
    nc = _NC_CACHE["nc"]
    in_maps = make_in_maps(inputs)
    res = run_bass_kernel_spmd(nc, in_maps, core_ids=list(range(8)))
    out = np.stack([np.ascontiguousarray(np.asarray(r["outT"]).T) for r in res.results], 0)
    return out.astype(np.float32)
```

```python
import contextlib
import numpy as np
import concourse.bass as bass
import concourse.mybir as mybir
from concourse.bass_utils import run_bass_kernel_spmd

F32 = mybir.dt.float32
BF16 = mybir.dt.bfloat16
AF = mybir.ActivationFunctionType
ALU = mybir.AluOpType

S = 4096
D = 1024
T = 512
NT = S // T
KC = D // 128
FF = 2816
FC = FF // 128
NH = 8
NKV = 2
EPS = 1e-6
ARENA = 208896

ENGS = ("pe", "act", "dve", "pool", "sp")
SIG_ROTATE = 6000
DMA_POOL = {"sp": 24, "pool": 12, "act": 8}


class Res:
    def __init__(self, name=""):
        self.name = name
        self.last_w = None
        self.rd = {}
        self.rd_dma = []
        self.pending = []
        self.dead = False
        self.excl = False


class Op:
    __slots__ = ("eng", "fn", "deps", "dma", "sig", "sem", "val", "prev_same_sem")

    def __init__(self, eng, fn, dma):
        self.eng = eng
        self.fn = fn
        self.dma = dma
        self.deps = []
        self.sig = False
        self.sem = None
        self.val = 0
        self.prev_same_sem = None


class Prog:
    def __init__(self, nc):
        self.nc = nc
        self.ops = {e: [] for e in ENGS}
        self.live = []
        self.arena = None

    def buf(self, off, dt, dims, name=""):
        esz = 2 if dt == BF16 else 4
        n = 1
        for d in dims:
            n *= d
        nbytes = n * esz
        assert off % 4 == 0 and nbytes % 4 == 0 and off + nbytes <= ARENA, (name, off, nbytes)
        r = Res(name)
        end = off + nbytes
        keep = []
        for (o, e, old) in self.live:
            if o < end and off < e:
                if old.last_w is not None:
                    r.pending.append(old.last_w)
                r.pending.extend(old.rd.values())
                r.pending.extend(old.rd_dma)
                r.pending.extend(old.pending)
                old.dead = True
            else:
                keep.append((o, e, old))
        keep.append((off, end, r))
        self.live = keep
        v = self.arena[:, off // 4:(off + nbytes) // 4]
        if dt == BF16:
            v = v.bitcast(BF16)
        if len(dims) == 2:
            v = v.rearrange("p (a b) -> p a b", b=dims[1])
        elif len(dims) == 3:
            v = v.rearrange("p (a b c) -> p a b c", b=dims[1], c=dims[2])
        return v, r

    def add(self, eng, fn, reads=(), writes=(), dma=False):
        op = Op(eng, fn, dma)
        deps = []
        for r in reads:
            assert not r.dead, r.name
            if r.last_w is not None:
                deps.append(r.last_w)
            deps.extend(r.pending)
            if r.excl:
                for e2, o2 in r.rd.items():
                    if e2 != eng:
                        deps.append(o2)
        for w in writes:
            assert not w.dead, w.name
            if w.last_w is not None:
                deps.append(w.last_w)
            deps.extend(w.pending)
            deps.extend(w.rd.values())
            deps.extend(w.rd_dma)
        seen = set()
        for d in deps:
            if id(d) in seen or d is op:
                continue
            seen.add(id(d))
            if d.eng == "pe" and eng == "pe" and not d.dma and not dma:
                continue
            op.deps.append(d)
        for r in reads:
            if dma:
                r.rd_dma.append(op)
            else:
                r.rd[eng] = op
        for w in writes:
            w.last_w = op
            w.rd = {}
            w.rd_dma = []
            w.pending = []
        self.ops[eng].append(op)
        return op

    def dma(self, q, out, in_, reads=(), writes=()):
        return self.add(q, lambda e: e.dma_start(out=out, in_=in_), reads, writes, dma=True)

    def emit(self, final_ops=()):
        nc = self.nc
        for e in ENGS:
            for op in self.ops[e]:
                for d in op.deps:
                    d.sig = True
        for op in final_ops:
            op.sig = True
        with contextlib.ExitStack() as st:
            sem_cnt = 0

            def new_sem(tag):
                nonlocal sem_cnt
                sem_cnt += 1
                return st.enter_context(nc.semaphore(f"s_{tag}_{sem_cnt}"))

            for e in ENGS:
                cur = None
                cnt = 0
                pool = []
                pool_last = []
                pi = 0
                for op in self.ops[e]:
                    if op.dma:
                        if len(pool) < DMA_POOL[e]:
                            pool.append(new_sem("d" + e))
                            pool_last.append(None)
                        k = pi % DMA_POOL[e]
                        pi += 1
                        op.sem = pool[k]
                        prev = pool_last[k]
                        op.prev_same_sem = prev
                        op.val = (prev.val if prev is not None else 0) + 16
                        pool_last[k] = op
                    elif op.sig:
                        if cur is None or cnt >= SIG_ROTATE:
                            cur = new_sem(e)
                            cnt = 0
                        cnt += 1
                        op.sem = cur
                        op.val = cnt
            self.n_sems = sem_cnt
            final = list(final_ops)
            with nc.Block() as block:
                def run(ename, eng):
                    known = {}
                    for op in self.ops[ename]:
                        waits = {}
                        dl = list(op.deps)
                        if op.dma and op.prev_same_sem is not None:
                            dl.append(op.prev_same_sem)
                        for d in dl:
                            k = id(d.sem)
                            if known.get(k, 0) >= d.val:
                                continue
                            if k not in waits or waits[k][1] < d.val:
                                waits[k] = (d.sem, d.val)
                        for k, (s, v) in waits.items():
                            eng.wait_ge(s, v)
                            known[k] = v
                        ins = op.fn(eng)
                        if op.dma:
                            ins.then_inc(op.sem, 16)
                        elif op.sig:
                            ins.then_inc(op.sem, 1)
                    if ename == "sp":
                        for op in final:
                            eng.wait_ge(op.sem, op.val)

                @block.tensor
                def _(eng):
                    run("pe", eng)

                @block.scalar
                def _(eng):
                    run("act", eng)

                @block.vector
                def _(eng):
                    run("dve", eng)

                @block.gpsimd
                def _(eng):
                    run("pool", eng)

                @block.sync
                def _(eng):
                    run("sp", eng)


def _vec_cols():
    cols = {}
    cur = 0

    def reg(name, n):
        nonlocal cur
        cols[name] = cur
        cur += n

    reg("c", 8)
    for l in range(2):
        reg(f"adab{l}", 72)
        reg(f"ng{l}", 24)
        reg(f"qg{l}", 1)
        reg(f"kg{l}", 1)
        reg(f"cw{l}", 32)
        reg(f"cb{l}", 8)
        reg(f"ba{l}", 16)
        reg(f"bx{l}", 16)
        reg(f"lam{l}", 16)
    reg("fg", 8)
    return cols, cur


VC, NV = _vec_cols()
STATS = {}
ROPE_PERM = np.concatenate([np.arange(0, 128, 2), np.arange(1, 128, 2)])


def _pack_vec(inp, b):
    v = np.zeros((128, NV), np.float32)

    def put(name, arr):
        arr = np.asarray(arr, np.float32)
        v[:, VC[name]:VC[name] + arr.shape[1]] = arr

    put("c", inp["c"][b].reshape(8, 128).T)
    for l in range(2):
        put(f"adab{l}", inp["ada_b"][l].reshape(9, 8, 128).transpose(2, 0, 1).reshape(128, 72))
        put(f"ng{l}", inp["norm_g"][l].reshape(3, 8, 128).transpose(2, 0, 1).reshape(128, 24))
        put(f"qg{l}", inp["q_norm_g"][l][ROPE_PERM].reshape(128, 1))
        put(f"kg{l}", inp["k_norm_g"][l][ROPE_PERM].reshape(128, 1))
        put(f"cw{l}", inp["conv_w"][l].reshape(4, 8, 128).transpose(2, 0, 1).reshape(128, 32))
        put(f"cb{l}", inp["conv_b"][l].reshape(8, 128).T)
        put(f"ba{l}", inp["lru_ba"][l].transpose(2, 0, 1).reshape(128, 16))
        put(f"bx{l}", inp["lru_bx"][l].transpose(2, 0, 1).reshape(128, 16))
        put(f"lam{l}", inp["lru_lambda"][l].reshape(2, 8, 128).transpose(2, 0, 1).reshape(128, 16))
    put("fg", inp["final_g"].reshape(8, 128).T)
    return v


def _rope_tables():
    rows = S // 64
    row_ids = np.broadcast_to(np.arange(rows, dtype=np.float32)[:, None], (rows, 64)).reshape(S)
    col_ids = np.broadcast_to(np.arange(64, dtype=np.float32)[None, :], (rows, 64)).reshape(S)
    inv_freq = (np.float32(10000.0) ** (-np.arange(0, 64, 2, dtype=np.float32) / np.float32(64))).astype(np.float32)
    ang = np.concatenate([row_ids[:, None] * inv_freq, col_ids[:, None] * inv_freq], axis=-1).astype(np.float32)
    c = np.cos(ang).astype(np.float32).T
    s = np.sin(ang).astype(np.float32).T
    cos2 = np.concatenate([c, c], 0)
    sin2 = np.concatenate([-s, s], 0)
    return np.ascontiguousarray(cos2), np.ascontiguousarray(sin2)


def build_nc(stop_after=None, n_layers=2):
    nc = bass.Bass("TRN2", target_bir_lowering=False)

    def din(name, shape, dt=F32):
        return nc.dram_tensor(name, shape, dt, kind="ExternalInput").ap()

    xT_in = din("xT", [D, S])
    vec_in = din("vec", [128, NV])
    cos_in = din("cos2", [128, S])
    sin_in = din("sin2", [128, S])
    ada_w = din("ada_w", [2, D, 9 * D])
    ffn_up = [din("ffn1_up", [2, D, 2 * FF]), din("ffn2_up", [2, D, 2 * FF])]
    ffn_down = [din("ffn1_down", [2, FF, D]), din("ffn2_down", [2, FF, D])]
    w_in = din("w_in", [2, D, 5632])
    lru_wa = din("lru_wa", [2, 2, 8, 128, 128])
    lru_wx = din("lru_wx", [2, 2, 8, 128, 128])
    w_attn_o = din("w_attn_o", [2, D, D])
    w_lru_o = din("w_lru_o", [2, D, D])
    w_out = din("w_out", [2, D, D])
    outT = nc.dram_tensor("outT", [D, S], F32, kind="ExternalOutput").ap()

    def scr(name, shape, dt):
        return nc.dram_tensor(name, shape, dt).ap()

    xa = scr("xa", [D, S], F32)
    qT = scr("qT_s", [NH, 128, S], BF16)
    kT = scr("kT_s", [NKV, 128, S], BF16)
    Vd = scr("V_s", [S, 256], BF16)
    lxT = scr("lxT_s", [D, S], F32)
    lgT = scr("lgT_s", [D, S], F32)
    gaT = scr("gaT_s", [D, S], F32)
    glT = scr("glT_s", [D, S], F32)
    lruT = scr("lruT_s", [D, S], BF16)

    R_xa = [Res(f"xa{t}") for t in range(NT)]
    R_q = [[Res() for _ in range(NT)] for _ in range(NH)]
    R_k = [Res() for _ in range(NT)]
    R_v = [Res() for _ in range(NT)]
    R_lx = [Res() for _ in range(KC)]
    R_lg = [Res() for _ in range(KC)]
    R_ga = [[Res() for _ in range(NT)] for _ in range(KC)]
    R_gl = [[Res() for _ in range(NT)] for _ in range(KC)]
    R_lru = [Res() for _ in range(KC)]
    R_lx_t = [[Res() for _ in range(NT)] for _ in range(KC)]
    R_lg_t = [[Res() for _ in range(NT)] for _ in range(KC)]

    P = Prog(nc)
    with contextlib.ExitStack() as st:
        arena = st.enter_context(nc.sbuf_tensor("arena", [128, ARENA // 4], F32))
        P.arena = arena
        vec = st.enter_context(nc.sbuf_tensor("vec_sb", [128, NV], F32))
        R_vec = Res("vec")
        NDV = 144 + 3 * 48 + 32 + 8
        dv = st.enter_context(nc.sbuf_tensor("dv_sb", [128, NDV], F32))
        R_dv = Res("dv")
        ones = st.enter_context(nc.sbuf_tensor("ones_sb", [128, 128], BF16))
        R_ones = Res("ones")
        cact = st.enter_context(nc.sbuf_tensor("cact_sb", [128, 8], BF16))
        R_cact = Res("cact")
        PS = [st.enter_context(nc.psum_tensor(f"ps{i}", [128, 512], F32)) for i in range(8)]
        RPS = [Res(f"ps{i}") for i in range(8)]
        for r_ in RPS:
            r_.excl = True

        def vcol(name, i=0, n=1):
            return vec[:, VC[name] + i:VC[name] + i + n]

        DV_MOD, DV_A, DV_B, DV_G, DV_CL, DV_TMP = 0, 144, 192, 240, 288, 320

        def dvc(base, i, n=1):
            return dv[:, base + i:base + i + n]

        P.dma("sp", vec[:], vec_in, writes=[R_vec])
        P.add("dve", lambda e: e.memset(ones[:], 1.0), [], [R_ones])

        def mod_phase():
            P.add("act", lambda e: e.activation(out=cact[:], in_=vcol("c", 0, 8), func=AF.Silu), [R_vec], [R_cact])
            WOFF = 90112
            nbuf = 2
            wb = [P.buf(WOFF + i * 16384, BF16, [KC, 1024], f"adaw{i}") for i in range(nbuf)]
            it = 0
            for l in range(n_layers):
                for m in range(9):
                    wv, wr = wb[it % nbuf]
                    src = ada_w[l].rearrange("(k p) n -> p k n", p=128)[:, :, m * 1024:(m + 1) * 1024]
                    P.dma("pool", wv, src, writes=[wr])
                    pb = it % 2
                    for j in range(8):
                        for k in range(8):
                            P.add("pe", lambda e, wv=wv, j=j, k=k, pb=pb: e.matmul(
                                PS[pb][:, j:j + 1], lhsT=wv[:, k, j * 128:(j + 1) * 128], rhs=cact[:, k:k + 1],
                                start=(k == 0), stop=(k == 7)), [wr, R_cact], [RPS[pb]])
                    P.add("dve", lambda e, l=l, m=m, pb=pb: e.tensor_tensor(
                        out=dvc(DV_MOD, l * 72 + m * 8, 8), in0=PS[pb][:, 0:8], in1=vcol(f"adab{l}", m * 8, 8), op=ALU.add),
                        [RPS[pb], R_vec], [R_dv])
                    it += 1
            for l in range(n_layers):
                for s in range(3):
                    P.add("dve", lambda e, l=l, s=s: e.scalar_tensor_tensor(
                        out=dvc(DV_A, l * 24 + s * 8, 8), in0=dvc(DV_MOD, l * 72 + (3 * s + 1) * 8, 8), scalar=1.0,
                        in1=vcol(f"ng{l}", s * 8, 8), op0=ALU.add, op1=ALU.mult), [R_dv, R_vec], [R_dv])
                    P.add("dve", lambda e, l=l, s=s: e.tensor_copy(
                        out=dvc(DV_B, l * 24 + s * 8, 8), in_=dvc(DV_MOD, l * 72 + (3 * s) * 8, 8)), [R_dv], [R_dv])
                    P.add("dve", lambda e, l=l, s=s: e.tensor_scalar(
                        out=dvc(DV_G, l * 24 + s * 8, 8), in0=dvc(DV_MOD, l * 72 + (3 * s + 2) * 8, 8),
                        scalar1=(1.0 if s == 1 else 0.5), scalar2=0.0, op0=ALU.mult, op1=ALU.add), [R_dv], [R_dv])
                P.add("act", lambda e, l=l: e.activation(out=dvc(DV_CL, l * 16, 16), in_=vcol(f"lam{l}", 0, 16),
                                                         func=AF.Exp, scale=-1.0), [R_vec], [R_dv])
                P.add("act", lambda e, l=l: e.activation(out=dvc(DV_CL, l * 16, 16), in_=dvc(DV_CL, l * 16, 16),
                                                         func=AF.Ln, bias=1.0), [R_dv], [R_dv])
                P.add("dve", lambda e, l=l: e.tensor_scalar(out=dvc(DV_CL, l * 16, 16), in0=dvc(DV_CL, l * 16, 16),
                                                            scalar1=-8.0, scalar2=0.0, op0=ALU.mult, op1=ALU.add), [R_dv], [R_dv])

        def xview(dram, t):
            return dram.rearrange("(k p) s -> p k s", p=128)[:, :, t * T:(t + 1) * T]

        class Norm:
            def __init__(self, XN, r_xn, HT, r_ht, SQ, RSTD, r_rstd, a_base, b_base, ps_i):
                self.XN, self.r_xn, self.HT, self.r_ht, self.SQ = XN, r_xn, HT, r_ht, SQ
                self.RSTD, self.r_rstd, self.a_base, self.b_base, self.ps_i = RSTD, r_rstd, a_base, b_base, ps_i

            def A_sq(self, k):
                XN, r_xn = self.XN, self.r_xn
                sq, r_sq = self.SQ[k % 2]
                P.add("act", lambda e: e.activation(out=sq, in_=XN[:, k, :], func=AF.Square), [r_xn], [r_sq])

            def A_mm(self, k):
                sq, r_sq = self.SQ[k % 2]
                ps_i = self.ps_i
                P.add("pe", lambda e: e.matmul(PS[ps_i][:], lhsT=ones[:], rhs=sq, start=(k == 0), stop=(k == 7)),
                      [R_ones, r_sq], [RPS[ps_i]])

            def A_fin(self):
                RSTD, r_rstd, ps_i = self.RSTD, self.r_rstd, self.ps_i
                P.add("act", lambda e: e.activation(out=RSTD, in_=PS[ps_i][:], func=AF.Ln, scale=1.0 / D, bias=EPS), [RPS[ps_i]], [r_rstd])
                P.add("act", lambda e: e.activation(out=RSTD, in_=RSTD, func=AF.Exp, scale=-0.5), [r_rstd], [r_rstd])

            def B_step(self, k):
                XN, r_xn, HT, r_ht, RSTD, r_rstd = self.XN, self.r_xn, self.HT, self.r_ht, self.RSTD, self.r_rstd
                a_base, b_base = self.a_base, self.b_base
                P.add("dve", lambda e: e.scalar_tensor_tensor(out=XN[:, k, :], in0=XN[:, k, :], scalar=dvc(a_base, k), in1=RSTD,
                                                              op0=ALU.mult, op1=ALU.mult), [r_xn, r_rstd, R_dv], [r_xn])
                P.add("act", lambda e: e.activation(out=HT[:, k, :], in_=XN[:, k, :], func=AF.Identity, bias=dvc(b_base, k), scale=1.0),
                      [r_xn, R_dv], [r_ht])

            def all(self):
                for k in range(KC):
                    self.A_sq(k)
                    self.A_mm(k)
                self.A_fin()
                for k in range(KC):
                    self.B_step(k)

        def ffn_phase(l, which, src):
            s = 0 if which == 0 else 2
            up = ffn_up[which][l]
            down = ffn_down[which][l]
            WA = [P.buf(g * 8192, BF16, [KC, 512], f"WA{g}") for g in range(11)]
            WB = [P.buf(90112 + h * 22528, BF16, [11, 1024], f"WB{h}") for h in range(2)]
            o = 135168
            XNv, r_xn = P.buf(o, F32, [KC, T], "XN"); o += 16384
            XRv, r_xr = P.buf(o, F32, [KC, T], "XR"); o += 16384
            HTv, r_ht = P.buf(o, BF16, [KC, T], "HT"); o += 8192
            ATc = [P.buf(o + j * 1024, BF16, [T], f"ACTT{j}") for j in range(FC)]; o += 22528
            SQ = []
            for i in range(2):
                v, r = P.buf(o, BF16, [T], "SQ"); o += 1024
                SQ.append((v, r))
            RSTD, r_rstd = P.buf(o, F32, [T], "RSTD"); o += 2048
            SG = []
            for i in range(2):
                v, r = P.buf(o, F32, [T], "SG"); o += 2048
                SG.append((v, r))
            upv = up.rearrange("(k p) n -> p k n", p=128)
            for g in range(11):
                P.dma("pool", WA[g][0], upv[:, :, g * 512:(g + 1) * 512], writes=[WA[g][1]])
            dnv = down.rearrange("(c p) n -> p c n", p=128)
            for h in range(2):
                P.dma("pool", WB[h][0], dnv[:, h * 11:(h + 1) * 11, :], writes=[WB[h][1]])

            def wa_cols(c0):
                g = c0 // 512
                return WA[g][0], c0 % 512, WA[g][1]

            def rsrc(t):
                return R_xa[t] if src is xa else Res()

            nrm = Norm(XNv, r_xn, HTv, r_ht, SQ, RSTD, r_rstd, DV_A + l * 24 + s * 8, DV_B + l * 24 + s * 8, 6)
            P.dma("sp", XNv, xview(src, 0), reads=[rsrc(0)], writes=[r_xn])
            nrm.all()
            for t in range(NT):
                nxt = t + 1 < NT
                P.dma("sp", XRv, xview(src, t), reads=[rsrc(t)], writes=[r_xr])
                if nxt:
                    P.dma("sp", XNv, xview(src, t + 1), reads=[rsrc(t + 1)], writes=[r_xn])
                for j in range(FC):
                    pg, pu = (0, 1) if j % 2 == 0 else (2, 3)
                    kk = (j - 4) // 2 if (j >= 4 and j % 2 == 0 and j < 20) else None
                    if nxt and kk is not None:
                        nrm.A_sq(kk)
                    for (pb, c0) in ((pg, j * 128), (pu, FF + j * 128)):
                        wv, off, wr = wa_cols(c0)
                        for k in range(KC):
                            P.add("pe", lambda e, wv=wv, off=off, k=k, pb=pb: e.matmul(
                                PS[pb][:], lhsT=wv[:, k, off:off + 128], rhs=HTv[:, k, :], start=(k == 0), stop=(k == 7)),
                                [wr, r_ht], [RPS[pb]])
                    if nxt and kk is not None:
                        nrm.A_mm(kk)
                    sg, r_sg = SG[j % 2]
                    P.add("act", lambda e, sg=sg, pg=pg: e.activation(out=sg, in_=PS[pg][:], func=AF.Silu), [RPS[pg]], [r_sg])
                    P.add("dve", lambda e, sg=sg, pu=pu, j=j: e.tensor_tensor(out=ATc[j][0], in0=sg, in1=PS[pu][:], op=ALU.mult),
                          [r_sg, RPS[pu]], [ATc[j][1]])
                if nxt:
                    nrm.A_fin()
                for d in range(KC):
                    pb = 4 + d % 2
                    for c in range(FC):
                        wv, wr = WB[c // 11]
                        P.add("pe", lambda e, wv=wv, c=c, d=d, pb=pb: e.matmul(
                            PS[pb][:], lhsT=wv[:, c % 11, d * 128:(d + 1) * 128], rhs=ATc[c][0], start=(c == 0), stop=(c == FC - 1)),
                            [wr, ATc[c][1]], [RPS[pb]])
                    P.add("dve", lambda e, d=d, pb=pb: e.scalar_tensor_tensor(
                        out=XRv[:, d, :], in0=PS[pb][:], scalar=dvc(DV_G + l * 24 + s * 8, d), in1=XRv[:, d, :],
                        op0=ALU.mult, op1=ALU.add), [RPS[pb], R_dv, r_xr], [r_xr])
                    if nxt:
                        nrm.B_step(d)
                P.dma("sp", xview(xa, t), XRv, reads=[r_xr], writes=[R_xa[t]])

        def m1_phase(l):
            WA = [P.buf(g * 8192, BF16, [KC, 512], f"WA{g}") for g in range(11)]
            wv_in = w_in[l].rearrange("(k p) n -> p k n", p=128)
            for g in range(11):
                P.dma("pool", WA[g][0], wv_in[:, :, g * 512:(g + 1) * 512], writes=[WA[g][1]])
            o = 90112

            def alloc(dt, dims, name):
                nonlocal o
                v, r = P.buf(o, dt, dims, name)
                n = 1
                for d_ in dims:
                    n *= d_
                o += n * (2 if dt == BF16 else 4)
                return v, r

            XN = [alloc(F32, [KC, T], f"XN{i}") for i in range(2)]
            HT = [alloc(BF16, [KC, T], f"HT{i}") for i in range(2)]
            SQ = [alloc(BF16, [T], "SQ") for i in range(2)]
            RSTD, r_rstd = alloc(F32, [T], "RSTD")
            STG = [alloc(F32, [T], f"STG{i}") for i in range(4)]
            VS, r_vs = alloc(BF16, [4, 256], "VS")
            COS = [alloc(F32, [T], f"COS{i}") for i in range(2)]
            SIN = [alloc(F32, [T], f"SIN{i}") for i in range(2)]
            NHB = 3
            QS = [alloc(BF16, [T], f"QS{i}") for i in range(NHB)]
            SQH = [alloc(BF16, [T], f"SQH{i}") for i in range(NHB)]
            RSH = [alloc(F32, [T], f"RSH{i}") for i in range(NHB)]
            SW = [alloc(F32, [T], f"SW{i}") for i in range(NHB)]
            T1 = [alloc(F32, [T], f"T1{i}") for i in range(NHB)]

            def wa_cols(c0):
                g = c0 // 512
                return WA[g][0], c0 % 512, WA[g][1]

            a_b, b_b = DV_A + l * 24 + 8, DV_B + l * 24 + 8
            norms = [Norm(XN[i][0], XN[i][1], HT[i][0], HT[i][1], SQ, RSTD, r_rstd, a_b, b_b, 6) for i in range(2)]
            P.dma("sp", XN[0][0], xview(xa, 0), reads=[R_xa[0]], writes=[XN[0][1]])
            norms[0].all()
            stg_i = 0
            hi = 0
            for t in range(NT):
                nxt = t + 1 < NT
                HTv, r_ht = HT[t % 2]
                cosv, r_cos = COS[t % 2]
                sinv, r_sin = SIN[t % 2]
                P.dma("sp", cosv, cos_in[:, t * T:(t + 1) * T], writes=[r_cos])
                P.dma("sp", sinv, sin_in[:, t * T:(t + 1) * T], writes=[r_sin])
                if nxt:
                    nn = norms[(t + 1) % 2]
                    P.dma("sp", nn.XN, xview(xa, t + 1), reads=[R_xa[t + 1]], writes=[nn.r_xn])
                pending = None
                for j in range(44):
                    if j in (10, 11):
                        continue
                    pb = j % 4
                    if nxt and 12 <= j < 20:
                        nn.A_sq(j - 12)
                    wv, off, wr = wa_cols(j * 128)
                    for k in range(KC):
                        P.add("pe", lambda e, wv=wv, off=off, k=k, pb=pb, HTv=HTv: e.matmul(
                            PS[pb][:], lhsT=wv[:, k, off:off + 128], rhs=HTv[:, k, :], start=(k == 0), stop=(k == 7)),
                            [wr, r_ht], [RPS[pb]])
                    if nxt and 12 <= j < 20:
                        nn.A_mm(j - 12)
                    if nxt and j == 20:
                        nn.A_fin()
                    if nxt and 22 <= j < 30:
                        nn.B_step(j - 22)
                    if pending is not None:
                        pending()
                        pending = None
                    if j < 10:
                        i3 = hi % NHB
                        hi += 1
                        sqh, r_sqh = SQH[i3]
                        rsh, r_rsh = RSH[i3]
                        sw, r_sw = SW[i3]
                        t1, r_t1 = T1[i3]
                        qs, r_qs = QS[i3]
                        gname = f"qg{l}" if j < 8 else f"kg{l}"
                        gcol = vcol(gname)
                        P.add("act", lambda e, sqh=sqh, pb=pb: e.activation(out=sqh, in_=PS[pb][:], func=AF.Square), [RPS[pb]], [r_sqh])
                        P.add("act", lambda e, sw=sw, pb=pb, gcol=gcol: e.activation(out=sw[0:64, :], in_=PS[pb][64:128, :], func=AF.Identity,
                                                                                    scale=gcol[64:128, :]), [RPS[pb], R_vec], [r_sw])
                        P.add("act", lambda e, sw=sw, pb=pb, gcol=gcol: e.activation(out=sw[64:128, :], in_=PS[pb][0:64, :], func=AF.Identity,
                                                                                    scale=gcol[0:64, :]), [RPS[pb], R_vec], [r_sw])
                        P.add("dve", lambda e, t1=t1, pb=pb, gcol=gcol, cosv=cosv: e.scalar_tensor_tensor(
                            out=t1, in0=PS[pb][:], scalar=gcol, in1=cosv, op0=ALU.mult, op1=ALU.mult), [RPS[pb], R_vec, r_cos], [r_t1])

                        def fin(j=j, sqh=sqh, r_sqh=r_sqh, rsh=rsh, r_rsh=r_rsh, sw=sw, r_sw=r_sw,
                                t1=t1, r_t1=r_t1, qs=qs, r_qs=r_qs, t=t, sinv=sinv, r_sin=r_sin):
                            P.add("pe", lambda e: e.matmul(PS[7][:], lhsT=ones[:], rhs=sqh, start=True, stop=True), [R_ones, r_sqh], [RPS[7]])
                            P.add("act", lambda e: e.activation(out=rsh, in_=PS[7][:], func=AF.Ln, scale=1.0 / 128, bias=EPS), [RPS[7]], [r_rsh])
                            P.add("act", lambda e: e.activation(out=rsh, in_=rsh, func=AF.Exp, scale=-0.5), [r_rsh], [r_rsh])
                            P.add("dve", lambda e: e.tensor_tensor(out=sw, in0=sw, in1=sinv, op=ALU.mult), [r_sw, r_sin], [r_sw])
                            P.add("dve", lambda e: e.tensor_tensor(out=t1, in0=t1, in1=sw, op=ALU.add), [r_t1, r_sw], [r_t1])
                            P.add("dve", lambda e: e.tensor_tensor(out=qs, in0=t1, in1=rsh, op=ALU.mult), [r_t1, r_rsh], [r_qs])
                            if j < 8:
                                P.dma("sp", qT[j][:, t * T:(t + 1) * T], qs, reads=[r_qs], writes=[R_q[j][t]])
                            else:
                                P.dma("sp", kT[j - 8][:, t * T:(t + 1) * T], qs, reads=[r_qs], writes=[R_k[t]])
                        pending = fin
                    else:
                        stg, r_stg = STG[stg_i % 4]
                        stg_i += 1
                        if j < 20:
                            kk = j - 12
                            P.add("act", lambda e, stg=stg, pb=pb: e.activation(out=stg, in_=PS[pb][:], func=AF.Copy), [RPS[pb]], [r_stg])
                            P.dma("sp", lxT[kk * 128:(kk + 1) * 128, t * T:(t + 1) * T], stg, reads=[r_stg], writes=[R_lx_t[kk][t]])
                        elif j < 28:
                            kk = j - 20
                            P.add("act", lambda e, stg=stg, pb=pb: e.activation(out=stg, in_=PS[pb][:], func=AF.Gelu), [RPS[pb]], [r_stg])
                            P.dma("sp", lgT[kk * 128:(kk + 1) * 128, t * T:(t + 1) * T], stg, reads=[r_stg], writes=[R_lg_t[kk][t]])
                        else:
                            kk = (j - 28) % 8
                            dst, rr = (gaT, R_ga) if j < 36 else (glT, R_gl)
                            P.add("act", lambda e, stg=stg, pb=pb: e.activation(out=stg, in_=PS[pb][:], func=AF.Sigmoid), [RPS[pb]], [r_stg])
                            P.dma("sp", dst[kk * 128:(kk + 1) * 128, t * T:(t + 1) * T], stg, reads=[r_stg], writes=[rr[kk][t]])
                if pending is not None:
                    pending()
                for blk in range(4):
                    pb = 4 + blk % 2
                    wv, off, wr = wa_cols(1280)
                    for k in range(KC):
                        P.add("pe", lambda e, wv=wv, off=off, k=k, pb=pb, blk=blk, HTv=HTv: e.matmul(
                            PS[pb][:, 0:256], lhsT=HTv[:, k, blk * 128:(blk + 1) * 128], rhs=wv[:, k, off:off + 256],
                            start=(k == 0), stop=(k == 7)), [wr, r_ht], [RPS[pb]])
                    P.add("dve", lambda e, pb=pb, blk=blk: e.tensor_copy(out=VS[:, blk, :], in_=PS[pb][:, 0:256]), [RPS[pb]], [r_vs])
                P.dma("sp", Vd.rearrange("(b p) c -> p b c", p=128)[:, t * 4:(t + 1) * 4, :], VS, reads=[r_vs], writes=[R_v[t]])

        def m2_phase(l):
            o = 90112

            def alloc(dt, dims, name):
                nonlocal o
                v, r = P.buf(o, dt, dims, name)
                n = 1
                for d_ in dims:
                    n *= d_
                o += n * (2 if dt == BF16 else 4)
                return v, r

            WG, r_wg = alloc(BF16, [32, 128], "WG")
            LXP, r_lxp = alloc(F32, [S + 4], "LXP")
            XC, r_xc = alloc(F32, [S], "XC")
            XCB, r_xcb = alloc(BF16, [S], "XCB")
            Av, r_a = alloc(F32, [S], "A")
            Uv, r_u = alloc(F32, [S], "U")
            S0 = alloc(F32, [S], "S0")
            S1 = alloc(F32, [S], "S1")
            P.dma("pool", WG[:, 0:16, :], lru_wa[l].rearrange("d b i j -> i (d b) j"), writes=[r_wg])
            P.dma("pool", WG[:, 16:32, :], lru_wx[l].rearrange("d b i j -> i (d b) j"), writes=[r_wg])
            P.add("dve", lambda e: e.memset(LXP[:, 0:2], 0.0), [], [r_lxp])
            P.add("dve", lambda e: e.memset(LXP[:, S + 2:S + 4], 0.0), [r_lxp], [r_lxp])
            for b in range(KC):
                P.dma("sp", LXP[:, 2:S + 2], lxT[b * 128:(b + 1) * 128, :], reads=[R_lx_t[b][t] for t in range(NT)] + [r_lxp], writes=[r_lxp])
                P.add("dve", lambda e, b=b: e.tensor_scalar(out=XC, in0=LXP[:, 0:S], scalar1=vcol(f"cw{l}", 0 * 8 + b),
                                                            scalar2=vcol(f"cb{l}", b), op0=ALU.mult, op1=ALU.add), [r_lxp, R_vec], [r_xc])
                for j in range(1, 4):
                    P.add("dve", lambda e, b=b, j=j: e.scalar_tensor_tensor(out=XC, in0=LXP[:, j:j + S], scalar=vcol(f"cw{l}", j * 8 + b),
                                                                             in1=XC, op0=ALU.mult, op1=ALU.add), [r_lxp, R_vec, r_xc], [r_xc])
                P.add("act", lambda e: e.activation(out=XCB, in_=XC, func=AF.Copy), [r_xc], [r_xcb])
                for dr in range(2):
                    Hv, r_h = (S0, S1)[dr]
                    for t in range(NT):
                        pa, px = (0, 1) if t % 2 == 0 else (2, 3)
                        sl = slice(t * T, (t + 1) * T)
                        P.add("pe", lambda e, pa=pa, sl=sl, dr=dr, b=b: e.matmul(PS[pa][:], lhsT=WG[:, dr * 8 + b, :], rhs=XCB[:, sl],
                                                                               start=True, stop=True), [r_wg, r_xcb], [RPS[pa]])
                        P.add("pe", lambda e, px=px, sl=sl, dr=dr, b=b: e.matmul(PS[px][:], lhsT=WG[:, 16 + dr * 8 + b, :], rhs=XCB[:, sl],
                                                                               start=True, stop=True), [r_wg, r_xcb], [RPS[px]])
                        P.add("act", lambda e, pa=pa, sl=sl, dr=dr, b=b: e.activation(out=Av[:, sl], in_=PS[pa][:], func=AF.Sigmoid,
                                                                                     bias=vcol(f"ba{l}", dr * 8 + b)), [RPS[pa], R_vec], [r_a])
                        P.add("act", lambda e, px=px, sl=sl, dr=dr, b=b: e.activation(out=Uv[:, sl], in_=PS[px][:], func=AF.Sigmoid,
                                                                                     bias=vcol(f"bx{l}", dr * 8 + b)), [RPS[px], R_vec], [r_u])
                    P.add("act", lambda e, dr=dr, b=b: e.activation(out=Av, in_=Av, func=AF.Exp, scale=dvc(DV_CL, l * 16 + dr * 8 + b)),
                          [r_a, R_dv], [r_a])
                    P.add("act", lambda e, Hv=Hv: e.activation(out=Hv, in_=Av, func=AF.Square), [r_a], [r_h])
                    P.add("act", lambda e, Hv=Hv: e.activation(out=Hv, in_=Hv, func=AF.Sqrt, scale=-1.0, bias=1.0), [r_h], [r_h])
                    P.add("dve", lambda e: e.tensor_tensor(out=Uv, in0=Uv, in1=XC, op=ALU.mult), [r_u, r_xc], [r_u])
                    P.add("dve", lambda e, Hv=Hv: e.tensor_tensor(out=Uv, in0=Uv, in1=Hv, op=ALU.mult), [r_u, r_h], [r_u])
                    if dr == 0:
                        P.add("dve", lambda e, Hv=Hv: e.tensor_tensor_scan(out=Hv, data0=Av, data1=Uv, initial=0.0, op0=ALU.mult, op1=ALU.add),
                              [r_a, r_u], [r_h])
                    else:
                        P.add("dve", lambda e, Hv=Hv: e.tensor_tensor_scan(out=Hv[:, ::-1], data0=Av[:, ::-1], data1=Uv[:, ::-1], initial=0.0,
                                                                         op0=ALU.mult, op1=ALU.add), [r_a, r_u], [r_h])
                P.add("dve", lambda e: e.tensor_tensor(out=S0[0], in0=S0[0], in1=S1[0], op=ALU.add), [S0[1], S1[1]], [S0[1]])
                P.dma("sp", Av, lgT[b * 128:(b + 1) * 128, :], reads=[R_lg_t[b][t] for t in range(NT)], writes=[r_a])
                P.add("dve", lambda e: e.tensor_tensor(out=XCB, in0=S0[0], in1=Av, op=ALU.mult), [S0[1], r_a, r_xcb], [r_xcb])
                P.dma("sp", lruT[b * 128:(b + 1) * 128, :], XCB, reads=[r_xcb], writes=[R_lru[b]])

        def m3_phase(l):
            o = 0

            def alloc(dt, dims, name):
                nonlocal o
                v, r = P.buf(o, dt, dims, name)
                n = 1
                for d_ in dims:
                    n *= d_
                o += n * (2 if dt == BF16 else 4)
                return v, r

            WAO, r_wao = alloc(BF16, [KC, D], "WAO")
            WLO, r_wlo = alloc(BF16, [KC, D], "WLO")
            WO, r_wo = alloc(BF16, [KC, D], "WO")
            KT, r_kt = alloc(BF16, [NKV, S], "KT")
            VV, r_vv = alloc(BF16, [32, 256], "VV")
            QB = [alloc(BF16, [T], f"QB{i}") for i in range(2)]
            PT = [alloc(BF16, [T], f"PT{i}") for i in range(3)]
            ATT = [alloc(BF16, [T], f"ATT{i}") for i in range(KC)]
            LRT, r_lrt = alloc(BF16, [KC, T], "LRT")
            GA, r_ga = alloc(F32, [KC, T], "GA")
            GL, r_gl = alloc(F32, [KC, T], "GL")
            MT = [alloc(BF16, [T], f"MT{i}") for i in range(KC)]
            XR, r_xr = alloc(F32, [KC, T], "XR")
            REC, r_rec = alloc(F32, [T], "REC")
            M1 = [alloc(F32, [T], f"M1{i}") for i in range(2)]
            M2 = [alloc(F32, [T], f"M2{i}") for i in range(2)]
            P.dma("pool", WAO, w_attn_o[l].rearrange("(k p) n -> p k n", p=128), writes=[r_wao])
            P.dma("pool", WLO, w_lru_o[l].rearrange("(k p) n -> p k n", p=128), writes=[r_wlo])
            P.dma("pool", WO, w_out[l].rearrange("(k p) n -> p k n", p=128), writes=[r_wo])
            P.dma("sp", KT, kT.rearrange("g p s -> p g s"), reads=R_k, writes=[r_kt])
            P.dma("sp", VV, Vd.rearrange("(b p) c -> p b c", p=128), reads=R_v, writes=[r_vv])
            scale = 128.0 ** -0.5
            QB3 = QB + [alloc(BF16, [T], "QB2")]
            seq = [(t, h) for t in range(NT) for h in range(NH)]

            def qload(i):
                t_, h_ = seq[i]
                qb_, r_qb_ = QB3[i % 3]
                P.dma("sp", qb_, qT[h_][:, t_ * T:(t_ + 1) * T], reads=[R_q[h_][t_]], writes=[r_qb_])

            qload(0)
            qload(1)
            qi = 0
            for t in range(NT):
                tsl = slice(t * T, (t + 1) * T)
                for h in range(NH):
                    if h == 2:
                        P.dma("sp", LRT, lruT.rearrange("(k p) s -> p k s", p=128)[:, :, tsl], reads=R_lru, writes=[r_lrt])
                        P.dma("sp", GA, gaT.rearrange("(k p) s -> p k s", p=128)[:, :, tsl], reads=[R_ga[kk][t] for kk in range(KC)], writes=[r_ga])
                        P.dma("sp", GL, glT.rearrange("(k p) s -> p k s", p=128)[:, :, tsl], reads=[R_gl[kk][t] for kk in range(KC)], writes=[r_gl])
                        P.dma("sp", XR, xview(xa, t), reads=[R_xa[t]], writes=[r_xr])
                    g = h // 4
                    qb, r_qb = QB3[qi % 3]
                    po, psm = (4, 5) if qi % 2 == 0 else (6, 7)
                    if qi + 2 < len(seq):
                        qload(qi + 2)
                    qi += 1

                    def qk(c, g=g, qb=qb, r_qb=r_qb):
                        pb = c % 2
                        P.add("pe", lambda e: e.matmul(PS[pb][:], lhsT=KT[:, g, c * 128:(c + 1) * 128], rhs=qb, start=True, stop=True),
                              [r_kt, r_qb], [RPS[pb]])
                        pt, r_pt = PT[c % 3]
                        P.add("act", lambda e: e.activation(out=pt, in_=PS[pb][:], func=AF.Exp, scale=scale), [RPS[pb]], [r_pt])

                    def pv(c, g=g, po=po, psm=psm):
                        pt, r_pt = PT[c % 3]
                        P.add("pe", lambda e: e.matmul(PS[po][:], lhsT=VV[:, c, g * 128:(g + 1) * 128], rhs=pt, start=(c == 0), stop=(c == 31)),
                              [r_vv, r_pt], [RPS[po]])
                        P.add("pe", lambda e: e.matmul(PS[psm][:], lhsT=ones[:], rhs=pt, start=(c == 0), stop=(c == 31)),
                              [R_ones, r_pt], [RPS[psm]])

                    qk(0)
                    for c in range(32):
                        if c + 1 < 32:
                            qk(c + 1)
                        pv(c)
                    P.add("dve", lambda e, psm=psm: e.reciprocal(out=REC, in_=PS[psm][:]), [RPS[psm]], [r_rec])
                    P.add("dve", lambda e, po=po, h=h: e.tensor_tensor(out=ATT[h][0], in0=PS[po][:], in1=REC, op=ALU.mult),
                          [RPS[po], r_rec], [ATT[h][1]])
                for d in range(KC):
                    pa, pl = (0, 1) if d % 2 == 0 else (2, 3)
                    dsl = slice(d * 128, (d + 1) * 128)
                    for k in range(KC):
                        P.add("pe", lambda e, pa=pa, k=k, dsl=dsl: e.matmul(PS[pa][:], lhsT=WAO[:, k, dsl], rhs=ATT[k][0], start=(k == 0), stop=(k == 7)),
                              [r_wao, ATT[k][1]], [RPS[pa]])
                    for k in range(KC):
                        P.add("pe", lambda e, pl=pl, k=k, dsl=dsl: e.matmul(PS[pl][:], lhsT=WLO[:, k, dsl], rhs=LRT[:, k, :], start=(k == 0), stop=(k == 7)),
                              [r_wlo, r_lrt], [RPS[pl]])
                    m1, r_m1 = M1[d % 2]
                    m2, r_m2 = M2[d % 2]
                    P.add("dve", lambda e, m1=m1, pa=pa, d=d: e.tensor_tensor(out=m1, in0=PS[pa][:], in1=GA[:, d, :], op=ALU.mult), [RPS[pa], r_ga], [r_m1])
                    P.add("dve", lambda e, m2=m2, pl=pl, d=d: e.tensor_tensor(out=m2, in0=PS[pl][:], in1=GL[:, d, :], op=ALU.mult), [RPS[pl], r_gl], [r_m2])
                    P.add("pool", lambda e, m1=m1, m2=m2, d=d: e.tensor_tensor(out=MT[d][0], in0=m1, in1=m2, op=ALU.add), [r_m1, r_m2], [MT[d][1]])
                for d in range(KC):
                    pb = d % 2
                    dsl = slice(d * 128, (d + 1) * 128)
                    for k in range(KC):
                        P.add("pe", lambda e, pb=pb, k=k, dsl=dsl: e.matmul(PS[pb][:], lhsT=WO[:, k, dsl], rhs=MT[k][0], start=(k == 0), stop=(k == 7)),
                              [r_wo, MT[k][1]], [RPS[pb]])
                    P.add("dve", lambda e, pb=pb, d=d: e.scalar_tensor_tensor(out=XR[:, d, :], in0=PS[pb][:], scalar=dvc(DV_G + l * 24 + 8, d),
                                                                             in1=XR[:, d, :], op0=ALU.mult, op1=ALU.add), [RPS[pb], R_dv, r_xr], [r_xr])
                P.dma("sp", xview(xa, t), XR, reads=[r_xr], writes=[R_xa[t]])

        def final_phase():
            o = 90112
            XN = []
            for i in range(2):
                XN.append(P.buf(o, F32, [KC, T], f"FXN{i}")); o += 16384
            SQ = []
            for i in range(2):
                SQ.append(P.buf(o, BF16, [T], "FSQ")); o += 1024
            RSTD, r_rstd = P.buf(o, F32, [T], "FRSTD"); o += 2048
            fin = []
            for t in range(NT):
                XNv, r_xn = XN[t % 2]
                P.dma("sp", XNv, xview(xa, t), reads=[R_xa[t]], writes=[r_xn])
                for k in range(KC):
                    sq, r_sq = SQ[k % 2]
                    P.add("act", lambda e, k=k, sq=sq, XNv=XNv: e.activation(out=sq, in_=XNv[:, k, :], func=AF.Square), [r_xn], [r_sq])
                    P.add("pe", lambda e, k=k, sq=sq: e.matmul(PS[6][:], lhsT=ones[:], rhs=sq, start=(k == 0), stop=(k == 7)),
                          [R_ones, r_sq], [RPS[6]])
                P.add("act", lambda e: e.activation(out=RSTD, in_=PS[6][:], func=AF.Ln, scale=1.0 / D, bias=EPS), [RPS[6]], [r_rstd])
                P.add("act", lambda e: e.activation(out=RSTD, in_=RSTD, func=AF.Exp, scale=-0.5), [r_rstd], [r_rstd])
                for k in range(KC):
                    P.add("dve", lambda e, k=k, XNv=XNv: e.scalar_tensor_tensor(out=XNv[:, k, :], in0=XNv[:, k, :], scalar=vcol("fg", k),
                                                                           in1=RSTD, op0=ALU.mult, op1=ALU.mult), [r_xn, r_rstd, R_vec], [r_xn])
                fin.append(P.dma("sp", xview(outT, t), XNv, reads=[r_xn]))
            return fin

        def copy_out():
            fin = []
            for t in range(NT):
                fin.append(P.dma("sp", outT[:, t * T:(t + 1) * T], xa[:, t * T:(t + 1) * T], reads=[R_xa[t]]))
            return fin

        phases = []
        for l in range(n_layers):
            phases += [("ffn", l, 0), ("m1", l), ("m2", l), ("m3", l), ("ffn", l, 1)]
        mod_phase()
        fin = None
        for i, ph in enumerate(phases):
            if ph[0] == "ffn":
                ffn_phase(ph[1], ph[2], xT_in if i == 0 else xa)
            elif ph[0] == "m1":
                m1_phase(ph[1])
            elif ph[0] == "m2":
                m2_phase(ph[1])
            elif ph[0] == "m3":
                m3_phase(ph[1])
            if stop_after is not None and i == stop_after:
                fin = copy_out()
                break
        if fin is None:
            fin = final_phase()
        P.emit(fin)
        STATS.clear()
        STATS.update({e: len(P.ops[e]) for e in ENGS})
        STATS["sems"] = P.n_sems
    return nc


def make_in_maps(inputs):
    inp = {k: np.asarray(v) for k, v in inputs.items()}
    cos2, sin2 = _rope_tables()
    w_in = np.array(inp["w_in"], np.float32, copy=True)
    for h in range(NH + NKV):
        w_in[:, :, h * 128:(h + 1) * 128] = inp["w_in"][:, :, h * 128 + ROPE_PERM]
    shared = {
        "cos2": cos2, "sin2": sin2,
        "ada_w": np.ascontiguousarray(inp["ada_w"], np.float32),
        "ffn1_up": np.ascontiguousarray(inp["ffn1_up"], np.float32),
        "ffn2_up": np.ascontiguousarray(inp["ffn2_up"], np.float32),
        "ffn1_down": np.ascontiguousarray(inp["ffn1_down"], np.float32),
        "ffn2_down": np.ascontiguousarray(inp["ffn2_down"], np.float32),
        "w_in": w_in,
        "lru_wa": np.ascontiguousarray(inp["lru_wa"], np.float32),
        "lru_wx": np.ascontiguousarray(inp["lru_wx"], np.float32),
        "w_attn_o": np.ascontiguousarray(inp["w_attn_o"], np.float32),
        "w_lru_o": np.ascontiguousarray(inp["w_lru_o"], np.float32),
        "w_out": np.ascontiguousarray(inp["w_out"], np.float32),
    }
    maps = []
    for b in range(inp["x"].shape[0]):
        m = dict(shared)
        m["xT"] = np.ascontiguousarray(inp["x"][b].T, np.float32)
        m["vec"] = _pack_vec(inp, b)
        maps.append(m)
    return maps


_NC_CACHE = {}


def kernel(**inputs):
    if "nc" not in _NC_CACHE:
        _NC_CACHE["nc"] = build_nc()
    nc = _NC_CACHE["nc"]
    in_maps = make_in_maps(inputs)
    res = run_bass_kernel_spmd(nc, in_maps, core_ids=list(range(8)))
    out = np.stack([np.ascontiguousarray(np.asarray(r["outT"]).T) for r in res.results], 0)
    return out.astype(np.float32)
```

```python
import contextlib
import numpy as np
import concourse.bass as bass
import concourse.mybir as mybir
from concourse.bass_utils import run_bass_kernel_spmd

F32 = mybir.dt.float32
BF16 = mybir.dt.bfloat16
AF = mybir.ActivationFunctionType
ALU = mybir.AluOpType

S = 4096
D = 1024
T = 512
NT = S // T
KC = D // 128
FF = 2816
FC = FF // 128
NH = 8
NKV = 2
EPS = 1e-6
ARENA = 208896

ENGS = ("pe", "act", "dve", "pool", "sp")
SIG_ROTATE = 6000
DMA_POOL = {"sp": 24, "pool": 12, "act": 8}


class Res:
    def __init__(self, name=""):
        self.name = name
        self.last_w = None
        self.rd = {}
        self.rd_dma = []
        self.pending = []
        self.dead = False
        self.excl = False


class Op:
    __slots__ = ("eng", "fn", "deps", "dma", "sig", "sem", "val", "prev_same_sem")

    def __init__(self, eng, fn, dma):
        self.eng = eng
        self.fn = fn
        self.dma = dma
        self.deps = []
        self.sig = False
        self.sem = None
        self.val = 0
        self.prev_same_sem = None


class Prog:
    def __init__(self, nc):
        self.nc = nc
        self.ops = {e: [] for e in ENGS}
        self.live = []
        self.arena = None

    def buf(self, off, dt, dims, name=""):
        esz = 2 if dt == BF16 else 4
        n = 1
        for d in dims:
            n *= d
        nbytes = n * esz
        assert off % 4 == 0 and nbytes % 4 == 0 and off + nbytes <= ARENA, (name, off, nbytes)
        r = Res(name)
        end = off + nbytes
        keep = []
        for (o, e, old) in self.live:
            if o < end and off < e:
                if old.last_w is not None:
                    r.pending.append(old.last_w)
                r.pending.extend(old.rd.values())
                r.pending.extend(old.rd_dma)
                r.pending.extend(old.pending)
                old.dead = True
            else:
                keep.append((o, e, old))
        keep.append((off, end, r))
        self.live = keep
        v = self.arena[:, off // 4:(off + nbytes) // 4]
        if dt == BF16:
            v = v.bitcast(BF16)
        if len(dims) == 2:
            v = v.rearrange("p (a b) -> p a b", b=dims[1])
        elif len(dims) == 3:
            v = v.rearrange("p (a b c) -> p a b c", b=dims[1], c=dims[2])
        return v, r

    def add(self, eng, fn, reads=(), writes=(), dma=False):
        op = Op(eng, fn, dma)
        deps = []
        for r in reads:
            assert not r.dead, r.name
            if r.last_w is not None:
                deps.append(r.last_w)
            deps.extend(r.pending)
            if r.excl:
                for e2, o2 in r.rd.items():
                    if e2 != eng:
                        deps.append(o2)
        for w in writes:
            assert not w.dead, w.name
            if w.last_w is not None:
                deps.append(w.last_w)
            deps.extend(w.pending)
            deps.extend(w.rd.values())
            deps.extend(w.rd_dma)
        seen = set()
        for d in deps:
            if id(d) in seen or d is op:
                continue
            seen.add(id(d))
            if d.eng == "pe" and eng == "pe" and not d.dma and not dma:
                continue
            op.deps.append(d)
        for r in reads:
            if dma:
                r.rd_dma.append(op)
            else:
                r.rd[eng] = op
        for w in writes:
            w.last_w = op
            w.rd = {}
            w.rd_dma = []
            w.pending = []
        self.ops[eng].append(op)
        return op

    def dma(self, q, out, in_, reads=(), writes=()):
        return self.add(q, lambda e: e.dma_start(out=out, in_=in_), reads, writes, dma=True)

    def emit(self, final_ops=()):
        nc = self.nc
        for e in ENGS:
            for op in self.ops[e]:
                for d in op.deps:
                    d.sig = True
        for op in final_ops:
            op.sig = True
        with contextlib.ExitStack() as st:
            sem_cnt = 0

            def new_sem(tag):
                nonlocal sem_cnt
                sem_cnt += 1
                return st.enter_context(nc.semaphore(f"s_{tag}_{sem_cnt}"))

            for e in ENGS:
                cur = None
                cnt = 0
                pool = []
                pool_last = []
                pi = 0
                for op in self.ops[e]:
                    if op.dma:
                        if len(pool) < DMA_POOL[e]:
                            pool.append(new_sem("d" + e))
                            pool_last.append(None)
                        k = pi % DMA_POOL[e]
                        pi += 1
                        op.sem = pool[k]
                        prev = pool_last[k]
                        op.prev_same_sem = prev
                        op.val = (prev.val if prev is not None else 0) + 16
                        pool_last[k] = op
                    elif op.sig:
                        if cur is None or cnt >= SIG_ROTATE:
                            cur = new_sem(e)
                            cnt = 0
                        cnt += 1
                        op.sem = cur
                        op.val = cnt
            self.n_sems = sem_cnt
            final = list(final_ops)
            with nc.Block() as block:
                def run(ename, eng):
                    known = {}
                    for op in self.ops[ename]:
                        waits = {}
                        dl = list(op.deps)
                        if op.dma and op.prev_same_sem is not None:
                            dl.append(op.prev_same_sem)
                        for d in dl:
                            k = id(d.sem)
                            if known.get(k, 0) >= d.val:
                                continue
                            if k not in waits or waits[k][1] < d.val:
                                waits[k] = (d.sem, d.val)
                        for k, (s, v) in waits.items():
                            eng.wait_ge(s, v)
                            known[k] = v
                        ins = op.fn(eng)
                        if op.dma:
                            ins.then_inc(op.sem, 16)
                        elif op.sig:
                            ins.then_inc(op.sem, 1)
                    if ename == "sp":
                        for op in final:
                            eng.wait_ge(op.sem, op.val)

                @block.tensor
                def _(eng):
                    run("pe", eng)

                @block.scalar
                def _(eng):
                    run("act", eng)

                @block.vector
                def _(eng):
                    run("dve", eng)

                @block.gpsimd
                def _(eng):
                    run("pool", eng)

                @block.sync
                def _(eng):
                    run("sp", eng)


def _vec_cols():
    cols = {}
    cur = 0

    def reg(name, n):
        nonlocal cur
        cols[name] = cur
        cur += n

    reg("c", 8)
    for l in range(2):
        reg(f"adab{l}", 72)
        reg(f"ng{l}", 24)
        reg(f"qg{l}", 1)
        reg(f"kg{l}", 1)
        reg(f"cw{l}", 32)
        reg(f"cb{l}", 8)
        reg(f"ba{l}", 16)
        reg(f"bx{l}", 16)
        reg(f"lam{l}", 16)
    reg("fg", 8)
    return cols, cur


VC, NV = _vec_cols()
STATS = {}
ROPE_PERM = np.concatenate([np.arange(0, 128, 2), np.arange(1, 128, 2)])


def _pack_vec(inp, b):
    v = np.zeros((128, NV), np.float32)

    def put(name, arr):
        arr = np.asarray(arr, np.float32)
        v[:, VC[name]:VC[name] + arr.shape[1]] = arr

    put("c", inp["c"][b].reshape(8, 128).T)
    for l in range(2):
        put(f"adab{l}", inp["ada_b"][l].reshape(9, 8, 128).transpose(2, 0, 1).reshape(128, 72))
        put(f"ng{l}", inp["norm_g"][l].reshape(3, 8, 128).transpose(2, 0, 1).reshape(128, 24))
        put(f"qg{l}", inp["q_norm_g"][l][ROPE_PERM].reshape(128, 1))
        put(f"kg{l}", inp["k_norm_g"][l][ROPE_PERM].reshape(128, 1))
        put(f"cw{l}", inp["conv_w"][l].reshape(4, 8, 128).transpose(2, 0, 1).reshape(128, 32))
        put(f"cb{l}", inp["conv_b"][l].reshape(8, 128).T)
        put(f"ba{l}", inp["lru_ba"][l].transpose(2, 0, 1).reshape(128, 16))
        put(f"bx{l}", inp["lru_bx"][l].transpose(2, 0, 1).reshape(128, 16))
        put(f"lam{l}", inp["lru_lambda"][l].reshape(2, 8, 128).transpose(2, 0, 1).reshape(128, 16))
    put("fg", inp["final_g"].reshape(8, 128).T)
    return v


def _rope_tables():
    rows = S // 64
    row_ids = np.broadcast_to(np.arange(rows, dtype=np.float32)[:, None], (rows, 64)).reshape(S)
    col_ids = np.broadcast_to(np.arange(64, dtype=np.float32)[None, :], (rows, 64)).reshape(S)
    inv_freq = (np.float32(10000.0) ** (-np.arange(0, 64, 2, dtype=np.float32) / np.float32(64))).astype(np.float32)
    ang = np.concatenate([row_ids[:, None] * inv_freq, col_ids[:, None] * inv_freq], axis=-1).astype(np.float32)
    c = np.cos(ang).astype(np.float32).T
    s = np.sin(ang).astype(np.float32).T
    cos2 = np.concatenate([c, c], 0)
    sin2 = np.concatenate([-s, s], 0)
    return np.ascontiguousarray(cos2), np.ascontiguousarray(sin2)


def build_nc(stop_after=None, n_layers=2):
    nc = bass.Bass("TRN2", target_bir_lowering=False)

    def din(name, shape, dt=F32):
        return nc.dram_tensor(name, shape, dt, kind="ExternalInput").ap()

    xT_in = din("xT", [D, S])
    vec_in = din("vec", [128, NV])
    cos_in = din("cos2", [128, S])
    sin_in = din("sin2", [128, S])
    ada_w = din("ada_w", [2, D, 9 * D])
    ffn_up = [din("ffn1_up", [2, D, 2 * FF]), din("ffn2_up", [2, D, 2 * FF])]
    ffn_down = [din("ffn1_down", [2, FF, D]), din("ffn2_down", [2, FF, D])]
    w_in = din("w_in", [2, D, 5632])
    lru_wa = din("lru_wa", [2, 2, 8, 128, 128])
    lru_wx = din("lru_wx", [2, 2, 8, 128, 128])
    w_attn_o = din("w_attn_o", [2, D, D])
    w_lru_o = din("w_lru_o", [2, D, D])
    w_out = din("w_out", [2, D, D])
    outT = nc.dram_tensor("outT", [D, S], F32, kind="ExternalOutput").ap()

    def scr(name, shape, dt):
        return nc.dram_tensor(name, shape, dt).ap()

    xa = scr("xa", [D, S], F32)
    qT = scr("qT_s", [NH, 128, S], BF16)
    kT = scr("kT_s", [NKV, 128, S], BF16)
    Vd = scr("V_s", [S, 256], BF16)
    lxT = scr("lxT_s", [D, S], F32)
    lgT = scr("lgT_s", [D, S], F32)
    gaT = scr("gaT_s", [D, S], F32)
    glT = scr("glT_s", [D, S], F32)
    lruT = scr("lruT_s", [D, S], BF16)
    attnT = scr("attnT_s", [D, S], BF16)

    R_xa = [[Res(f"xa{t}_{d}") for d in range(KC)] for t in range(NT)]
    R_q = [[Res() for _ in range(NT)] for _ in range(NH)]
    R_k = [Res() for _ in range(NT)]
    R_v = [Res() for _ in range(NT)]
    R_lx = [Res() for _ in range(KC)]
    R_lg = [Res() for _ in range(KC)]
    R_ga = [[Res() for _ in range(NT)] for _ in range(KC)]
    R_gl = [[Res() for _ in range(NT)] for _ in range(KC)]
    R_lru = [Res() for _ in range(KC)]
    R_att = [[Res() for _ in range(NT)] for _ in range(NH)]
    R_lx_t = [[Res() for _ in range(NT)] for _ in range(KC)]
    R_lg_t = [[Res() for _ in range(NT)] for _ in range(KC)]

    P = Prog(nc)
    with contextlib.ExitStack() as st:
        arena = st.enter_context(nc.sbuf_tensor("arena", [128, ARENA // 4], F32))
        P.arena = arena
        vec = st.enter_context(nc.sbuf_tensor("vec_sb", [128, NV], F32))
        R_vec = Res("vec")
        NDV = 144 + 3 * 48 + 32 + 8
        dv = st.enter_context(nc.sbuf_tensor("dv_sb", [128, NDV], F32))
        R_dv = Res("dv")
        ones = st.enter_context(nc.sbuf_tensor("ones_sb", [128, 128], BF16))
        R_ones = Res("ones")
        cact = st.enter_context(nc.sbuf_tensor("cact_sb", [128, 8], BF16))
        R_cact = Res("cact")
        PS = [st.enter_context(nc.psum_tensor(f"ps{i}", [128, 512], F32)) for i in range(8)]
        RPS = [Res(f"ps{i}") for i in range(8)]
        for r_ in RPS:
            r_.excl = True

        def vcol(name, i=0, n=1):
            return vec[:, VC[name] + i:VC[name] + i + n]

        DV_MOD, DV_A, DV_B, DV_G, DV_CL, DV_TMP = 0, 144, 192, 240, 288, 320

        def dvc(base, i, n=1):
            return dv[:, base + i:base + i + n]

        P.dma("sp", vec[:], vec_in, writes=[R_vec])
        P.add("dve", lambda e: e.memset(ones[:], 1.0), [], [R_ones])

        def mod_phase():
            P.add("act", lambda e: e.activation(out=cact[:], in_=vcol("c", 0, 8), func=AF.Silu), [R_vec], [R_cact])
            WOFF = 90112
            nbuf = 2
            wb = [P.buf(WOFF + i * 16384, BF16, [KC, 1024], f"adaw{i}") for i in range(nbuf)]
            it = 0
            for l in range(n_layers):
                for m in range(9):
                    wv, wr = wb[it % nbuf]
                    src = ada_w[l].rearrange("(k p) n -> p k n", p=128)[:, :, m * 1024:(m + 1) * 1024]
                    P.dma("pool", wv, src, writes=[wr])
                    pb = it % 2
                    for j in range(8):
                        for k in range(8):
                            P.add("pe", lambda e, wv=wv, j=j, k=k, pb=pb: e.matmul(
                                PS[pb][:, j:j + 1], lhsT=wv[:, k, j * 128:(j + 1) * 128], rhs=cact[:, k:k + 1],
                                start=(k == 0), stop=(k == 7)), [wr, R_cact], [RPS[pb]])
                    P.add("dve", lambda e, l=l, m=m, pb=pb: e.tensor_tensor(
                        out=dvc(DV_MOD, l * 72 + m * 8, 8), in0=PS[pb][:, 0:8], in1=vcol(f"adab{l}", m * 8, 8), op=ALU.add),
                        [RPS[pb], R_vec], [R_dv])
                    it += 1
            for l in range(n_layers):
                for s in range(3):
                    P.add("dve", lambda e, l=l, s=s: e.scalar_tensor_tensor(
                        out=dvc(DV_A, l * 24 + s * 8, 8), in0=dvc(DV_MOD, l * 72 + (3 * s + 1) * 8, 8), scalar=1.0,
                        in1=vcol(f"ng{l}", s * 8, 8), op0=ALU.add, op1=ALU.mult), [R_dv, R_vec], [R_dv])
                    P.add("dve", lambda e, l=l, s=s: e.tensor_copy(
                        out=dvc(DV_B, l * 24 + s * 8, 8), in_=dvc(DV_MOD, l * 72 + (3 * s) * 8, 8)), [R_dv], [R_dv])
                    P.add("dve", lambda e, l=l, s=s: e.tensor_scalar(
                        out=dvc(DV_G, l * 24 + s * 8, 8), in0=dvc(DV_MOD, l * 72 + (3 * s + 2) * 8, 8),
                        scalar1=(1.0 if s == 1 else 0.5), scalar2=0.0, op0=ALU.mult, op1=ALU.add), [R_dv], [R_dv])
                P.add("act", lambda e, l=l: e.activation(out=dvc(DV_CL, l * 16, 16), in_=vcol(f"lam{l}", 0, 16),
                                                         func=AF.Exp, scale=-1.0), [R_vec], [R_dv])
                P.add("act", lambda e, l=l: e.activation(out=dvc(DV_CL, l * 16, 16), in_=dvc(DV_CL, l * 16, 16),
                                                         func=AF.Ln, bias=1.0), [R_dv], [R_dv])
                P.add("dve", lambda e, l=l: e.tensor_scalar(out=dvc(DV_CL, l * 16, 16), in0=dvc(DV_CL, l * 16, 16),
                                                            scalar1=-8.0, scalar2=0.0, op0=ALU.mult, op1=ALU.add), [R_dv], [R_dv])

        def xview(dram, t):
            return dram.rearrange("(k p) s -> p k s", p=128)[:, :, t * T:(t + 1) * T]

        class Norm:
            def __init__(self, XN, r_xn, HT, r_ht, SQ, RSTD, r_rstd, a_base, b_base, ps_i):
                self.XN, self.r_xn, self.HT, self.r_ht, self.SQ = XN, r_xn, HT, r_ht, SQ
                self.RSTD, self.r_rstd, self.a_base, self.b_base, self.ps_i = RSTD, r_rstd, a_base, b_base, ps_i

            def A_sq(self, k):
                XN, r_xn = self.XN, self.r_xn
                sq, r_sq = self.SQ[k % 2]
                P.add("act", lambda e: e.activation(out=sq, in_=XN[:, k, :], func=AF.Square), [r_xn], [r_sq])

            def A_mm(self, k):
                sq, r_sq = self.SQ[k % 2]
                ps_i = self.ps_i
                P.add("pe", lambda e: e.matmul(PS[ps_i][:], lhsT=ones[:], rhs=sq, start=(k == 0), stop=(k == 7)),
                      [R_ones, r_sq], [RPS[ps_i]])

            def A_fin(self):
                RSTD, r_rstd, ps_i = self.RSTD, self.r_rstd, self.ps_i
                P.add("act", lambda e: e.activation(out=RSTD, in_=PS[ps_i][:], func=AF.Ln, scale=1.0 / D, bias=EPS), [RPS[ps_i]], [r_rstd])
                P.add("act", lambda e: e.activation(out=RSTD, in_=RSTD, func=AF.Exp, scale=-0.5), [r_rstd], [r_rstd])

            def B_step(self, k):
                XN, r_xn, HT, r_ht, RSTD, r_rstd = self.XN, self.r_xn, self.HT, self.r_ht, self.RSTD, self.r_rstd
                a_base, b_base = self.a_base, self.b_base
                P.add("dve", lambda e: e.scalar_tensor_tensor(out=XN[:, k, :], in0=XN[:, k, :], scalar=dvc(a_base, k), in1=RSTD,
                                                              op0=ALU.mult, op1=ALU.mult), [r_xn, r_rstd, R_dv], [r_xn])
                P.add("act", lambda e: e.activation(out=HT[:, k, :], in_=XN[:, k, :], func=AF.Identity, bias=dvc(b_base, k), scale=1.0),
                      [r_xn, R_dv], [r_ht])

            def all(self):
                for k in range(KC):
                    self.A_sq(k)
                    self.A_mm(k)
                self.A_fin()
                for k in range(KC):
                    self.B_step(k)

        def ffn_phase(l, which, src):
            s = 0 if which == 0 else 2
            up = ffn_up[which][l]
            down = ffn_down[which][l]
            WA = [P.buf(g * 8192, BF16, [KC, 512], f"WA{g}") for g in range(11)]
            WB = [P.buf(90112 + h * 22528, BF16, [11, 1024], f"WB{h}") for h in range(2)]
            o = 135168
            XNv, r_xn = P.buf(o, F32, [KC, T], "XN"); o += 16384
            XRv, r_xr = P.buf(o, F32, [KC, T], "XR"); o += 16384
            HTv, r_ht = P.buf(o, BF16, [KC, T], "HT"); o += 8192
            ATc = [P.buf(o + j * 1024, BF16, [T], f"ACTT{j}") for j in range(FC)]; o += 22528
            SQ = []
            for i in range(2):
                v, r = P.buf(o, BF16, [T], "SQ"); o += 1024
                SQ.append((v, r))
            RSTD, r_rstd = P.buf(o, F32, [T], "RSTD"); o += 2048
            SG = []
            for i in range(2):
                v, r = P.buf(o, F32, [T], "SG"); o += 2048
                SG.append((v, r))
            upv = up.rearrange("(k p) n -> p k n", p=128)
            for g in range(11):
                P.dma("pool", WA[g][0], upv[:, :, g * 512:(g + 1) * 512], writes=[WA[g][1]])
            dnv = down.rearrange("(c p) n -> p c n", p=128)
            for h in range(2):
                P.dma("pool", WB[h][0], dnv[:, h * 11:(h + 1) * 11, :], writes=[WB[h][1]])

            def wa_cols(c0):
                g = c0 // 512
                return WA[g][0], c0 % 512, WA[g][1]

            def rsrc(t):
                return R_xa[t] if src is xa else [Res()]

            nrm = Norm(XNv, r_xn, HTv, r_ht, SQ, RSTD, r_rstd, DV_A + l * 24 + s * 8, DV_B + l * 24 + s * 8, 6)
            P.dma("sp", XNv, xview(src, 0), reads=rsrc(0), writes=[r_xn])
            nrm.all()
            for t in range(NT):
                nxt = t + 1 < NT
                P.dma("sp", XRv, xview(src, t), reads=rsrc(t), writes=[r_xr])
                if nxt:
                    P.dma("sp", XNv, xview(src, t + 1), reads=rsrc(t + 1), writes=[r_xn])
                for j in range(FC):
                    pg, pu = (0, 1) if j % 2 == 0 else (2, 3)
                    kk = (j - 4) // 2 if (j >= 4 and j % 2 == 0 and j < 20) else None
                    if nxt and kk is not None:
                        nrm.A_sq(kk)
                    for (pb, c0) in ((pg, j * 128), (pu, FF + j * 128)):
                        wv, off, wr = wa_cols(c0)
                        for k in range(KC):
                            P.add("pe", lambda e, wv=wv, off=off, k=k, pb=pb: e.matmul(
                                PS[pb][:], lhsT=wv[:, k, off:off + 128], rhs=HTv[:, k, :], start=(k == 0), stop=(k == 7)),
                                [wr, r_ht], [RPS[pb]])
                    if nxt and kk is not None:
                        nrm.A_mm(kk)
                    sg, r_sg = SG[j % 2]
                    P.add("act", lambda e, sg=sg, pg=pg: e.activation(out=sg, in_=PS[pg][:], func=AF.Silu), [RPS[pg]], [r_sg])
                    P.add("dve", lambda e, sg=sg, pu=pu, j=j: e.tensor_tensor(out=ATc[j][0], in0=sg, in1=PS[pu][:], op=ALU.mult),
                          [r_sg, RPS[pu]], [ATc[j][1]])
                if nxt:
                    nrm.A_fin()
                for d in range(KC):
                    pb = 4 + d % 2
                    for c in range(FC):
                        wv, wr = WB[c // 11]
                        P.add("pe", lambda e, wv=wv, c=c, d=d, pb=pb: e.matmul(
                            PS[pb][:], lhsT=wv[:, c % 11, d * 128:(d + 1) * 128], rhs=ATc[c][0], start=(c == 0), stop=(c == FC - 1)),
                            [wr, ATc[c][1]], [RPS[pb]])
                    P.add("dve", lambda e, d=d, pb=pb: e.scalar_tensor_tensor(
                        out=XRv[:, d, :], in0=PS[pb][:], scalar=dvc(DV_G + l * 24 + s * 8, d), in1=XRv[:, d, :],
                        op0=ALU.mult, op1=ALU.add), [RPS[pb], R_dv, r_xr], [r_xr])
                    if nxt:
                        nrm.B_step(d)
                P.dma("sp", xview(xa, t), XRv, reads=[r_xr], writes=R_xa[t])

        def m1_phase(l):
            WA = [P.buf(g * 8192, BF16, [KC, 512], f"WA{g}") for g in range(11)]
            wv_in = w_in[l].rearrange("(k p) n -> p k n", p=128)
            for g in range(11):
                P.dma("pool", WA[g][0], wv_in[:, :, g * 512:(g + 1) * 512], writes=[WA[g][1]])
            o = 90112

            def alloc(dt, dims, name):
                nonlocal o
                v, r = P.buf(o, dt, dims, name)
                n = 1
                for d_ in dims:
                    n *= d_
                o += n * (2 if dt == BF16 else 4)
                return v, r

            XN = [alloc(F32, [KC, T], f"XN{i}") for i in range(2)]
            HT = [alloc(BF16, [KC, T], f"HT{i}") for i in range(2)]
            SQ = [alloc(BF16, [T], "SQ") for i in range(2)]
            RSTD, r_rstd = alloc(F32, [T], "RSTD")
            STG = [alloc(F32, [T], f"STG{i}") for i in range(4)]
            VS, r_vs = alloc(BF16, [4, 256], "VS")
            COS = [alloc(F32, [T], f"COS{i}") for i in range(2)]
            SIN = [alloc(F32, [T], f"SIN{i}") for i in range(2)]
            NHB = 3
            QS = [alloc(BF16, [T], f"QS{i}") for i in range(NHB)]
            SQH = [alloc(BF16, [T], f"SQH{i}") for i in range(NHB)]
            RSH = [alloc(F32, [T], f"RSH{i}") for i in range(NHB)]
            SW = [alloc(F32, [T], f"SW{i}") for i in range(NHB)]
            T1 = [alloc(F32, [T], f"T1{i}") for i in range(NHB)]

            def wa_cols(c0):
                g = c0 // 512
                return WA[g][0], c0 % 512, WA[g][1]

            a_b, b_b = DV_A + l * 24 + 8, DV_B + l * 24 + 8
            norms = [Norm(XN[i][0], XN[i][1], HT[i][0], HT[i][1], SQ, RSTD, r_rstd, a_b, b_b, 6) for i in range(2)]
            P.dma("sp", XN[0][0], xview(xa, 0), reads=R_xa[0], writes=[XN[0][1]])
            norms[0].all()
            stg_i = 0
            hi = 0
            for t in range(NT):
                nxt = t + 1 < NT
                HTv, r_ht = HT[t % 2]
                cosv, r_cos = COS[t % 2]
                sinv, r_sin = SIN[t % 2]
                P.dma("sp", cosv, cos_in[:, t * T:(t + 1) * T], writes=[r_cos])
                P.dma("sp", sinv, sin_in[:, t * T:(t + 1) * T], writes=[r_sin])
                if nxt:
                    nn = norms[(t + 1) % 2]
                    P.dma("sp", nn.XN, xview(xa, t + 1), reads=R_xa[t + 1], writes=[nn.r_xn])
                pending = None
                for j in range(44):
                    if j in (10, 11):
                        continue
                    pb = j % 4
                    if nxt and 12 <= j < 20:
                        nn.A_sq(j - 12)
                    wv, off, wr = wa_cols(j * 128)
                    for k in range(KC):
                        P.add("pe", lambda e, wv=wv, off=off, k=k, pb=pb, HTv=HTv: e.matmul(
                            PS[pb][:], lhsT=wv[:, k, off:off + 128], rhs=HTv[:, k, :], start=(k == 0), stop=(k == 7)),
                            [wr, r_ht], [RPS[pb]])
                    if nxt and 12 <= j < 20:
                        nn.A_mm(j - 12)
                    if nxt and j == 20:
                        nn.A_fin()
                    if nxt and 22 <= j < 30:
                        nn.B_step(j - 22)
                    if pending is not None:
                        pending()
                        pending = None
                    if j < 10:
                        i3 = hi % NHB
                        hi += 1
                        sqh, r_sqh = SQH[i3]
                        rsh, r_rsh = RSH[i3]
                        sw, r_sw = SW[i3]
                        t1, r_t1 = T1[i3]
                        qs, r_qs = QS[i3]
                        gname = f"qg{l}" if j < 8 else f"kg{l}"
                        gcol = vcol(gname)
                        P.add("act", lambda e, sqh=sqh, pb=pb: e.activation(out=sqh, in_=PS[pb][:], func=AF.Square), [RPS[pb]], [r_sqh])
                        P.add("act", lambda e, sw=sw, pb=pb, gcol=gcol: e.activation(out=sw[0:64, :], in_=PS[pb][64:128, :], func=AF.Identity,
                                                                                    scale=gcol[64:128, :]), [RPS[pb], R_vec], [r_sw])
                        P.add("act", lambda e, sw=sw, pb=pb, gcol=gcol: e.activation(out=sw[64:128, :], in_=PS[pb][0:64, :], func=AF.Identity,
                                                                                    scale=gcol[0:64, :]), [RPS[pb], R_vec], [r_sw])
                        P.add("dve", lambda e, t1=t1, pb=pb, gcol=gcol, cosv=cosv: e.scalar_tensor_tensor(
                            out=t1, in0=PS[pb][:], scalar=gcol, in1=cosv, op0=ALU.mult, op1=ALU.mult), [RPS[pb], R_vec, r_cos], [r_t1])

                        def fin(j=j, sqh=sqh, r_sqh=r_sqh, rsh=rsh, r_rsh=r_rsh, sw=sw, r_sw=r_sw,
                                t1=t1, r_t1=r_t1, qs=qs, r_qs=r_qs, t=t, sinv=sinv, r_sin=r_sin):
                            P.add("pe", lambda e: e.matmul(PS[7][:], lhsT=ones[:], rhs=sqh, start=True, stop=True), [R_ones, r_sqh], [RPS[7]])
                            P.add("act", lambda e: e.activation(out=rsh, in_=PS[7][:], func=AF.Ln, scale=1.0 / 128, bias=EPS), [RPS[7]], [r_rsh])
                            P.add("act", lambda e: e.activation(out=rsh, in_=rsh, func=AF.Exp, scale=-0.5), [r_rsh], [r_rsh])
                            P.add("dve", lambda e: e.tensor_tensor(out=sw, in0=sw, in1=sinv, op=ALU.mult), [r_sw, r_sin], [r_sw])
                            P.add("dve", lambda e: e.tensor_tensor(out=t1, in0=t1, in1=sw, op=ALU.add), [r_t1, r_sw], [r_t1])
                            P.add("dve", lambda e: e.tensor_tensor(out=qs, in0=t1, in1=rsh, op=ALU.mult), [r_t1, r_rsh], [r_qs])
                            if j < 8:
                                P.dma("sp", qT[j][:, t * T:(t + 1) * T], qs, reads=[r_qs], writes=[R_q[j][t]])
                            else:
                                P.dma("sp", kT[j - 8][:, t * T:(t + 1) * T], qs, reads=[r_qs], writes=[R_k[t]])
                        pending = fin
                    else:
                        stg, r_stg = STG[stg_i % 4]
                        stg_i += 1
                        if j < 20:
                            kk = j - 12
                            P.add("act", lambda e, stg=stg, pb=pb: e.activation(out=stg, in_=PS[pb][:], func=AF.Copy), [RPS[pb]], [r_stg])
                            P.dma("sp", lxT[kk * 128:(kk + 1) * 128, t * T:(t + 1) * T], stg, reads=[r_stg], writes=[R_lx_t[kk][t]])
                        elif j < 28:
                            kk = j - 20
                            P.add("act", lambda e, stg=stg, pb=pb: e.activation(out=stg, in_=PS[pb][:], func=AF.Gelu), [RPS[pb]], [r_stg])
                            P.dma("sp", lgT[kk * 128:(kk + 1) * 128, t * T:(t + 1) * T], stg, reads=[r_stg], writes=[R_lg_t[kk][t]])
                        else:
                            kk = (j - 28) % 8
                            dst, rr = (gaT, R_ga) if j < 36 else (glT, R_gl)
                            P.add("act", lambda e, stg=stg, pb=pb: e.activation(out=stg, in_=PS[pb][:], func=AF.Sigmoid), [RPS[pb]], [r_stg])
                            P.dma("sp", dst[kk * 128:(kk + 1) * 128, t * T:(t + 1) * T], stg, reads=[r_stg], writes=[rr[kk][t]])
                if pending is not None:
                    pending()
                for blk in range(4):
                    pb = 4 + blk % 2
                    wv, off, wr = wa_cols(1280)
                    for k in range(KC):
                        P.add("pe", lambda e, wv=wv, off=off, k=k, pb=pb, blk=blk, HTv=HTv: e.matmul(
                            PS[pb][:, 0:256], lhsT=HTv[:, k, blk * 128:(blk + 1) * 128], rhs=wv[:, k, off:off + 256],
                            start=(k == 0), stop=(k == 7)), [wr, r_ht], [RPS[pb]])
                    P.add("dve", lambda e, pb=pb, blk=blk: e.tensor_copy(out=VS[:, blk, :], in_=PS[pb][:, 0:256]), [RPS[pb]], [r_vs])
                P.dma("sp", Vd.rearrange("(b p) c -> p b c", p=128)[:, t * 4:(t + 1) * 4, :], VS, reads=[r_vs], writes=[R_v[t]])

        def m2_phase(l):
            o = 90112

            def alloc(dt, dims, name):
                nonlocal o
                v, r = P.buf(o, dt, dims, name)
                n = 1
                for d_ in dims:
                    n *= d_
                o += n * (2 if dt == BF16 else 4)
                return v, r

            WG, r_wg = alloc(BF16, [32, 128], "WG")
            LXP, r_lxp = alloc(F32, [S + 4], "LXP")
            XC, r_xc = alloc(F32, [S], "XC")
            XCB, r_xcb = alloc(BF16, [S], "XCB")
            Av, r_a = alloc(F32, [S], "A")
            Uv, r_u = alloc(F32, [S], "U")
            S0 = alloc(F32, [S], "S0")
            S1 = alloc(F32, [S], "S1")
            P.dma("pool", WG[:, 0:16, :], lru_wa[l].rearrange("d b i j -> i (d b) j"), writes=[r_wg])
            P.dma("pool", WG[:, 16:32, :], lru_wx[l].rearrange("d b i j -> i (d b) j"), writes=[r_wg])
            P.add("dve", lambda e: e.memset(LXP[:, 0:2], 0.0), [], [r_lxp])
            P.add("dve", lambda e: e.memset(LXP[:, S + 2:S + 4], 0.0), [r_lxp], [r_lxp])
            for b in range(KC):
                P.dma("sp", LXP[:, 2:S + 2], lxT[b * 128:(b + 1) * 128, :], reads=[R_lx_t[b][t] for t in range(NT)] + [r_lxp], writes=[r_lxp])
                P.add("dve", lambda e, b=b: e.tensor_scalar(out=XC, in0=LXP[:, 0:S], scalar1=vcol(f"cw{l}", 0 * 8 + b),
                                                            scalar2=vcol(f"cb{l}", b), op0=ALU.mult, op1=ALU.add), [r_lxp, R_vec], [r_xc])
                for j in range(1, 4):
                    P.add("dve", lambda e, b=b, j=j: e.scalar_tensor_tensor(out=XC, in0=LXP[:, j:j + S], scalar=vcol(f"cw{l}", j * 8 + b),
                                                                             in1=XC, op0=ALU.mult, op1=ALU.add), [r_lxp, R_vec, r_xc], [r_xc])
                P.add("act", lambda e: e.activation(out=XCB, in_=XC, func=AF.Copy), [r_xc], [r_xcb])
                for dr in range(2):
                    Hv, r_h = (S0, S1)[dr]
                    for t in range(NT):
                        pa, px = (0, 1) if t % 2 == 0 else (2, 3)
                        sl = slice(t * T, (t + 1) * T)
                        P.add("pe", lambda e, pa=pa, sl=sl, dr=dr, b=b: e.matmul(PS[pa][:], lhsT=WG[:, dr * 8 + b, :], rhs=XCB[:, sl],
                                                                               start=True, stop=True), [r_wg, r_xcb], [RPS[pa]])
                        P.add("pe", lambda e, px=px, sl=sl, dr=dr, b=b: e.matmul(PS[px][:], lhsT=WG[:, 16 + dr * 8 + b, :], rhs=XCB[:, sl],
                                                                               start=True, stop=True), [r_wg, r_xcb], [RPS[px]])
                        P.add("act", lambda e, pa=pa, sl=sl, dr=dr, b=b: e.activation(out=Av[:, sl], in_=PS[pa][:], func=AF.Sigmoid,
                                                                                     bias=vcol(f"ba{l}", dr * 8 + b)), [RPS[pa], R_vec], [r_a])
                        P.add("act", lambda e, px=px, sl=sl, dr=dr, b=b: e.activation(out=Uv[:, sl], in_=PS[px][:], func=AF.Sigmoid,
                                                                                     bias=vcol(f"bx{l}", dr * 8 + b)), [RPS[px], R_vec], [r_u])
                    P.add("act", lambda e, dr=dr, b=b: e.activation(out=Av, in_=Av, func=AF.Exp, scale=dvc(DV_CL, l * 16 + dr * 8 + b)),
                          [r_a, R_dv], [r_a])
                    P.add("act", lambda e, Hv=Hv: e.activation(out=Hv, in_=Av, func=AF.Square), [r_a], [r_h])
                    P.add("act", lambda e, Hv=Hv: e.activation(out=Hv, in_=Hv, func=AF.Sqrt, scale=-1.0, bias=1.0), [r_h], [r_h])
                    P.add("dve", lambda e: e.tensor_tensor(out=Uv, in0=Uv, in1=XC, op=ALU.mult), [r_u, r_xc], [r_u])
                    P.add("dve", lambda e, Hv=Hv: e.tensor_tensor(out=Uv, in0=Uv, in1=Hv, op=ALU.mult), [r_u, r_h], [r_u])
                    if dr == 0:
                        P.add("dve", lambda e, Hv=Hv: e.tensor_tensor_scan(out=Hv, data0=Av, data1=Uv, initial=0.0, op0=ALU.mult, op1=ALU.add),
                              [r_a, r_u], [r_h])
                    else:
                        P.add("dve", lambda e, Hv=Hv: e.tensor_tensor_scan(out=Hv[:, ::-1], data0=Av[:, ::-1], data1=Uv[:, ::-1], initial=0.0,
                                                                         op0=ALU.mult, op1=ALU.add), [r_a, r_u], [r_h])
                P.add("dve", lambda e: e.tensor_tensor(out=S0[0], in0=S0[0], in1=S1[0], op=ALU.add), [S0[1], S1[1]], [S0[1]])
                P.dma("sp", Av, lgT[b * 128:(b + 1) * 128, :], reads=[R_lg_t[b][t] for t in range(NT)], writes=[r_a])
                P.add("dve", lambda e: e.tensor_tensor(out=XCB, in0=S0[0], in1=Av, op=ALU.mult), [S0[1], r_a, r_xcb], [r_xcb])
                P.dma("sp", lruT[b * 128:(b + 1) * 128, :], XCB, reads=[r_xcb], writes=[R_lru[b]])

        def m3_phase(l):
            o = 0

            def alloc(dt, dims, name):
                nonlocal o
                v, r = P.buf(o, dt, dims, name)
                n = 1
                for d_ in dims:
                    n *= d_
                o += n * (2 if dt == BF16 else 4)
                return v, r

            WAO, r_wao = alloc(BF16, [KC, D], "WAO")
            WLO, r_wlo = alloc(BF16, [KC, D], "WLO")
            WO, r_wo = alloc(BF16, [KC, D], "WO")
            KT, r_kt = alloc(BF16, [NKV, S], "KT")
            VV, r_vv = alloc(BF16, [32, 256], "VV")
            QB = [alloc(BF16, [T], f"QB{i}") for i in range(2)]
            PT = [alloc(BF16, [T], f"PT{i}") for i in range(3)]
            ATT = [alloc(BF16, [T], f"ATT{i}") for i in range(KC)]
            LRT, r_lrt = alloc(BF16, [KC, T], "LRT")
            GA, r_ga = alloc(F32, [KC, T], "GA")
            GL, r_gl = alloc(F32, [KC, T], "GL")
            MT = [alloc(BF16, [T], f"MT{i}") for i in range(KC)]
            XR, r_xr = alloc(F32, [KC, T], "XR")
            REC, r_rec = alloc(F32, [T], "REC")
            M1 = [alloc(F32, [T], f"M1{i}") for i in range(2)]
            M2 = [alloc(F32, [T], f"M2{i}") for i in range(2)]
            P.dma("pool", WAO, w_attn_o[l].rearrange("(k p) n -> p k n", p=128), writes=[r_wao])
            P.dma("pool", WLO, w_lru_o[l].rearrange("(k p) n -> p k n", p=128), writes=[r_wlo])
            P.dma("pool", WO, w_out[l].rearrange("(k p) n -> p k n", p=128), writes=[r_wo])
            P.dma("sp", KT, kT.rearrange("g p s -> p g s"), reads=R_k, writes=[r_kt])
            P.dma("sp", VV, Vd.rearrange("(b p) c -> p b c", p=128), reads=R_v, writes=[r_vv])
            scale = 128.0 ** -0.5
            QB3 = QB + [alloc(BF16, [T], "QB2")]
            seq = [(t, h) for t in range(NT) for h in range(NH)]

            def qload(i):
                t_, h_ = seq[i]
                qb_, r_qb_ = QB3[i % 3]
                P.dma("sp", qb_, qT[h_][:, t_ * T:(t_ + 1) * T], reads=[R_q[h_][t_]], writes=[r_qb_])

            qload(0)
            qload(1)
            qi = 0
            for t in range(NT):
                tsl = slice(t * T, (t + 1) * T)
                for h in range(NH):
                    if h == 2:
                        P.dma("sp", LRT, lruT.rearrange("(k p) s -> p k s", p=128)[:, :, tsl], reads=R_lru, writes=[r_lrt])
                        P.dma("sp", GA, gaT.rearrange("(k p) s -> p k s", p=128)[:, :, tsl], reads=[R_ga[kk][t] for kk in range(KC)], writes=[r_ga])
                        P.dma("sp", GL, glT.rearrange("(k p) s -> p k s", p=128)[:, :, tsl], reads=[R_gl[kk][t] for kk in range(KC)], writes=[r_gl])
                        P.dma("sp", XR, xview(xa, t), reads=R_xa[t], writes=[r_xr])
                    g = h // 4
                    qb, r_qb = QB3[qi % 3]
                    po, psm = (4, 5) if qi % 2 == 0 else (6, 7)
                    if qi + 2 < len(seq):
                        qload(qi + 2)
                    qi += 1

                    def qk(c, g=g, qb=qb, r_qb=r_qb):
                        pb = c % 2
                        P.add("pe", lambda e: e.matmul(PS[pb][:], lhsT=KT[:, g, c * 128:(c + 1) * 128], rhs=qb, start=True, stop=True),
                              [r_kt, r_qb], [RPS[pb]])
                        pt, r_pt = PT[c % 3]
                        P.add("act", lambda e: e.activation(out=pt, in_=PS[pb][:], func=AF.Exp, scale=scale), [RPS[pb]], [r_pt])

                    def pv(c, g=g, po=po, psm=psm):
                        pt, r_pt = PT[c % 3]
                        P.add("pe", lambda e: e.matmul(PS[po][:], lhsT=VV[:, c, g * 128:(g + 1) * 128], rhs=pt, start=(c == 0), stop=(c == 31)),
                              [r_vv, r_pt], [RPS[po]])
                        P.add("pe", lambda e: e.matmul(PS[psm][:], lhsT=ones[:], rhs=pt, start=(c == 0), stop=(c == 31)),
                              [R_ones, r_pt], [RPS[psm]])

                    qk(0)
                    for c in range(32):
                        if c + 1 < 32:
                            qk(c + 1)
                        pv(c)
                    P.add("dve", lambda e, psm=psm: e.reciprocal(out=REC, in_=PS[psm][:]), [RPS[psm]], [r_rec])
                    P.add("dve", lambda e, po=po, h=h: e.tensor_tensor(out=ATT[h][0], in0=PS[po][:], in1=REC, op=ALU.mult),
                          [RPS[po], r_rec], [ATT[h][1]])
                for d in range(KC):
                    pa, pl = (0, 1) if d % 2 == 0 else (2, 3)
                    dsl = slice(d * 128, (d + 1) * 128)
                    for k in range(KC):
                        P.add("pe", lambda e, pa=pa, k=k, dsl=dsl: e.matmul(PS[pa][:], lhsT=WAO[:, k, dsl], rhs=ATT[k][0], start=(k == 0), stop=(k == 7)),
                              [r_wao, ATT[k][1]], [RPS[pa]])
                    for k in range(KC):
                        P.add("pe", lambda e, pl=pl, k=k, dsl=dsl: e.matmul(PS[pl][:], lhsT=WLO[:, k, dsl], rhs=LRT[:, k, :], start=(k == 0), stop=(k == 7)),
                              [r_wlo, r_lrt], [RPS[pl]])
                    m1, r_m1 = M1[d % 2]
                    m2, r_m2 = M2[d % 2]
                    P.add("dve", lambda e, m1=m1, pa=pa, d=d: e.tensor_tensor(out=m1, in0=PS[pa][:], in1=GA[:, d, :], op=ALU.mult), [RPS[pa], r_ga], [r_m1])
                    P.add("dve", lambda e, m2=m2, pl=pl, d=d: e.tensor_tensor(out=m2, in0=PS[pl][:], in1=GL[:, d, :], op=ALU.mult), [RPS[pl], r_gl], [r_m2])
                    P.add("pool", lambda e, m1=m1, m2=m2, d=d: e.tensor_tensor(out=MT[d][0], in0=m1, in1=m2, op=ALU.add), [r_m1, r_m2], [MT[d][1]])
                for d in range(KC):
                    pb = d % 2
                    dsl = slice(d * 128, (d + 1) * 128)
                    for k in range(KC):
                        P.add("pe", lambda e, pb=pb, k=k, dsl=dsl: e.matmul(PS[pb][:], lhsT=WO[:, k, dsl], rhs=MT[k][0], start=(k == 0), stop=(k == 7)),
                              [r_wo, MT[k][1]], [RPS[pb]])
                    P.add("dve", lambda e, pb=pb, d=d: e.scalar_tensor_tensor(out=XR[:, d, :], in0=PS[pb][:], scalar=dvc(DV_G + l * 24 + 8, d),
                                                                             in1=XR[:, d, :], op0=ALU.mult, op1=ALU.add), [RPS[pb], R_dv, r_xr], [r_xr])
                P.dma("sp", xview(xa, t), XR, reads=[r_xr], writes=R_xa[t])

        def m23_phase(l):
            o = 0

            def alloc(dt, dims, name):
                nonlocal o
                v, r = P.buf(o, dt, dims, name)
                n = 1
                for d_ in dims:
                    n *= d_
                o += n * (2 if dt == BF16 else 4)
                return v, r

            KT, r_kt = alloc(BF16, [NKV, S], "KT")
            VV, r_vv = alloc(BF16, [32, 256], "VV")
            QB3 = [alloc(BF16, [T], f"QB{i}") for i in range(3)]
            PT = [alloc(BF16, [T], f"PT{i}") for i in range(3)]
            REC, r_rec = alloc(F32, [T], "REC")
            ATS = [alloc(BF16, [T], f"ATS{i}") for i in range(4)]
            WG, r_wg = alloc(BF16, [32, 128], "WG")
            LXP, r_lxp = alloc(F32, [S + 4], "LXP")
            XC, r_xc = alloc(F32, [S], "XC")
            XCB, r_xcb = alloc(BF16, [S], "XCB")
            AU = [(alloc(F32, [S], f"A{i}"), alloc(F32, [S], f"U{i}")) for i in range(2)]
            SH = [alloc(F32, [S], f"S{i}") for i in range(2)]
            P.dma("sp", KT, kT.rearrange("g p s -> p g s"), reads=R_k, writes=[r_kt])
            P.dma("sp", VV, Vd.rearrange("(b p) c -> p b c", p=128), reads=R_v, writes=[r_vv])
            P.dma("pool", WG[:, 0:16, :], lru_wa[l].rearrange("d b i j -> i (d b) j"), writes=[r_wg])
            P.dma("pool", WG[:, 16:32, :], lru_wx[l].rearrange("d b i j -> i (d b) j"), writes=[r_wg])
            P.add("dve", lambda e: e.memset(LXP[:, 0:2], 0.0), [], [r_lxp])
            P.add("dve", lambda e: e.memset(LXP[:, S + 2:S + 4], 0.0), [r_lxp], [r_lxp])
            NPC = 4
            PCW = S // NPC

            def pieces():
                return [slice(i * PCW, (i + 1) * PCW) for i in range(NPC)]

            def lru_steps(b):
                s0 = []
                s0.append(lambda: P.dma("sp", LXP[:, 2:S + 2], lxT[b * 128:(b + 1) * 128, :],
                                        reads=[R_lx_t[b][t] for t in range(NT)] + [r_lxp], writes=[r_lxp]))
                s0.append(lambda: P.add("dve", lambda e: e.tensor_scalar(out=XC, in0=LXP[:, 0:S], scalar1=vcol(f"cw{l}", b), scalar2=vcol(f"cb{l}", b),
                                                                         op0=ALU.mult, op1=ALU.add), [r_lxp, R_vec], [r_xc]))
                for j in range(1, 4):
                    s0.append(lambda j=j: P.add("dve", lambda e: e.scalar_tensor_tensor(out=XC, in0=LXP[:, j:j + S], scalar=vcol(f"cw{l}", j * 8 + b),
                                                                                      in1=XC, op0=ALU.mult, op1=ALU.add), [r_lxp, R_vec, r_xc], [r_xc]))
                s0.append(lambda: P.add("pool", lambda e: e.tensor_copy(out=XCB, in_=XC), [r_xc], [r_xcb]))

                s2 = []
                for dr in range(2):
                    for t in range(NT):
                        def pair(dr=dr, t=t):
                            (Av, r_a), (Uv, r_u) = AU[dr]
                            pa, px = (2, 3)
                            sl = slice(t * T, (t + 1) * T)
                            P.add("pe", lambda e: e.matmul(PS[pa][:], lhsT=WG[:, dr * 8 + b, :], rhs=XCB[:, sl], start=True, stop=True),
                                  [r_wg, r_xcb], [RPS[pa]])
                            P.add("pe", lambda e: e.matmul(PS[px][:], lhsT=WG[:, 16 + dr * 8 + b, :], rhs=XCB[:, sl], start=True, stop=True),
                                  [r_wg, r_xcb], [RPS[px]])
                            P.add("act", lambda e: e.activation(out=Av[:, sl], in_=PS[pa][:], func=AF.Sigmoid, bias=vcol(f"ba{l}", dr * 8 + b)),
                                  [RPS[pa], R_vec], [r_a])
                            P.add("act", lambda e: e.activation(out=Uv[:, sl], in_=PS[px][:], func=AF.Sigmoid, bias=vcol(f"bx{l}", dr * 8 + b)),
                                  [RPS[px], R_vec], [r_u])
                        s2.append(pair)

                def chain(dr):
                    (Av, r_a), (Uv, r_u) = AU[dr]
                    Hv, r_h = SH[dr]
                    lst = []
                    for sl in pieces():
                        lst.append(lambda sl=sl: P.add("act", lambda e: e.activation(out=Av[:, sl], in_=Av[:, sl], func=AF.Exp,
                                                                                    scale=dvc(DV_CL, l * 16 + dr * 8 + b)), [r_a, R_dv], [r_a]))
                    lst.append(lambda: P.add("pool", lambda e: e.tensor_tensor(out=Hv, in0=Av, in1=Av, op=ALU.mult), [r_a], [r_h]))
                    for sl in pieces():
                        lst.append(lambda sl=sl: P.add("act", lambda e: e.activation(out=Hv[:, sl], in_=Hv[:, sl], func=AF.Ln, scale=-1.0, bias=1.0),
                                                       [r_h], [r_h]))
                        lst.append(lambda sl=sl: P.add("act", lambda e: e.activation(out=Hv[:, sl], in_=Hv[:, sl], func=AF.Exp, scale=0.5),
                                                       [r_h], [r_h]))
                    lst.append(lambda: P.add("dve", lambda e: e.tensor_tensor(out=Uv, in0=Uv, in1=XC, op=ALU.mult), [r_u, r_xc], [r_u]))
                    lst.append(lambda: P.add("dve", lambda e: e.tensor_tensor(out=Uv, in0=Uv, in1=Hv, op=ALU.mult), [r_u, r_h], [r_u]))
                    if dr == 0:
                        lst.append(lambda: P.add("dve", lambda e: e.tensor_tensor_scan(out=Hv, data0=Av, data1=Uv, initial=0.0, op0=ALU.mult, op1=ALU.add),
                                                 [r_a, r_u], [r_h]))
                    else:
                        lst.append(lambda: P.add("dve", lambda e: e.tensor_tensor_scan(out=Hv[:, ::-1], data0=Av[:, ::-1], data1=Uv[:, ::-1], initial=0.0,
                                                                                       op0=ALU.mult, op1=ALU.add), [r_a, r_u], [r_h]))
                    return lst

                def s5f():
                    (A0, r_a0), (U0, r_u0) = AU[0]
                    P.dma("sp", U0, lgT[b * 128:(b + 1) * 128, :], reads=[R_lg_t[b][t] for t in range(NT)], writes=[r_u0])
                    P.add("dve", lambda e: e.tensor_tensor(out=SH[0][0], in0=SH[0][0], in1=SH[1][0], op=ALU.add), [SH[0][1], SH[1][1]], [SH[0][1]])
                    P.add("dve", lambda e: e.tensor_tensor(out=XCB, in0=SH[0][0], in1=U0, op=ALU.mult), [SH[0][1], r_u0], [r_xcb])
                    P.dma("sp", lruT[b * 128:(b + 1) * 128, :], XCB, reads=[r_xcb], writes=[R_lru[b]])

                def s2all():
                    for f in s2:
                        f()
                return [s0, [], [s2all], chain(0), chain(1), [s5f]]

            steps = []
            for b in range(KC):
                steps += lru_steps(b)
            inline_q = []

            scale = 128.0 ** -0.5
            seq = [(t, h) for t in range(NT) for h in range(NH)]

            def qload(i):
                t_, h_ = seq[i]
                qb_, r_qb_ = QB3[i % 3]
                P.dma("sp", qb_, qT[h_][:, t_ * T:(t_ + 1) * T], reads=[R_q[h_][t_]], writes=[r_qb_])

            qload(0)
            qload(1)
            for qi, (t, h) in enumerate(seq):
                tsl = slice(t * T, (t + 1) * T)
                g = h // 4
                qb, r_qb = QB3[qi % 3]
                po, psm = (4, 5) if qi % 2 == 0 else (6, 7)
                if qi + 2 < len(seq):
                    qload(qi + 2)

                def qk(c, g=g, qb=qb, r_qb=r_qb):
                    pb = c % 2
                    P.add("pe", lambda e: e.matmul(PS[pb][:], lhsT=KT[:, g, c * 128:(c + 1) * 128], rhs=qb, start=True, stop=True),
                          [r_kt, r_qb], [RPS[pb]])
                    pt, r_pt = PT[c % 3]
                    P.add("act", lambda e: e.activation(out=pt, in_=PS[pb][:], func=AF.Exp, scale=scale), [RPS[pb]], [r_pt])

                def pv(c, g=g, po=po, psm=psm):
                    pt, r_pt = PT[c % 3]
                    P.add("pe", lambda e: e.matmul(PS[po][:], lhsT=VV[:, c, g * 128:(g + 1) * 128], rhs=pt, start=(c == 0), stop=(c == 31)),
                          [r_vv, r_pt], [RPS[po]])
                    P.add("pe", lambda e: e.matmul(PS[psm][:], lhsT=ones[:], rhs=pt, start=(c == 0), stop=(c == 31)),
                          [R_ones, r_pt], [RPS[psm]])

                qk(0)
                for c in range(32):
                    if c + 1 < 32:
                        qk(c + 1)
                    pv(c)
                    if inline_q:
                        inline_q.pop(0)()
                ats, r_ats = ATS[qi % 4]
                P.add("dve", lambda e, psm=psm: e.reciprocal(out=REC, in_=PS[psm][:]), [RPS[psm]], [r_rec])
                P.add("dve", lambda e, po=po, ats=ats: e.tensor_tensor(out=ats, in0=PS[po][:], in1=REC, op=ALU.mult), [RPS[po], r_rec], [r_ats])
                P.dma("sp", attnT[h * 128:(h + 1) * 128, tsl], ats, reads=[r_ats], writes=[R_att[h][t]])
                if qi < len(steps):
                    inline_q.extend(steps[qi])
            for i in range(len(seq), len(steps)):
                inline_q.extend(steps[i])
            while inline_q:
                inline_q.pop(0)()

        def m3b_phase(l):
            o = 90112

            def alloc(dt, dims, name):
                nonlocal o
                v, r = P.buf(o, dt, dims, name)
                n = 1
                for d_ in dims:
                    n *= d_
                o += n * (2 if dt == BF16 else 4)
                return v, r

            WAO, r_wao = alloc(BF16, [KC, D], "WAO")
            WLO, r_wlo = alloc(BF16, [KC, D], "WLO")
            WO, r_wo = alloc(BF16, [KC, D], "WO")
            ATT = [alloc(BF16, [KC, T], f"ATT{i}") for i in range(2)]
            LRT = [alloc(BF16, [KC, T], f"LRT{i}") for i in range(2)]
            MT = [alloc(BF16, [T], f"MT{i}") for i in range(KC)]
            GX = [alloc(F32, [T], f"GX{i}") for i in range(6)]
            XO = [alloc(F32, [T], f"XO{i}") for i in range(3)]
            M1 = [alloc(F32, [T], f"M1{i}") for i in range(2)]
            M2 = [alloc(F32, [T], f"M2{i}") for i in range(2)]
            P.dma("pool", WAO, w_attn_o[l].rearrange("(k p) n -> p k n", p=128), writes=[r_wao])
            P.dma("pool", WLO, w_lru_o[l].rearrange("(k p) n -> p k n", p=128), writes=[r_wlo])
            P.dma("pool", WO, w_out[l].rearrange("(k p) n -> p k n", p=128), writes=[r_wo])

            def loads(t):
                tsl = slice(t * T, (t + 1) * T)
                P.dma("sp", ATT[t % 2][0], attnT.rearrange("(k p) s -> p k s", p=128)[:, :, tsl], reads=[R_att[h][t] for h in range(NH)],
                      writes=[ATT[t % 2][1]])
                P.dma("sp", LRT[t % 2][0], lruT.rearrange("(k p) s -> p k s", p=128)[:, :, tsl], reads=R_lru, writes=[LRT[t % 2][1]])

            loads(0)
            gi = 0
            xi = 0
            for t in range(NT):
                tsl = slice(t * T, (t + 1) * T)
                if t + 1 < NT:
                    loads(t + 1)
                ATv, r_att = ATT[t % 2]
                LRv, r_lrt = LRT[t % 2]
                for d in range(KC):
                    pa, pl = (0, 1) if d % 2 == 0 else (2, 3)
                    dsl = slice(d * 128, (d + 1) * 128)
                    ga, r_ga = GX[gi % 6]
                    gl, r_gl = GX[(gi + 1) % 6]
                    gi += 2
                    P.dma("sp", ga, gaT[dsl, tsl], reads=[R_ga[d][t]], writes=[r_ga])
                    P.dma("sp", gl, glT[dsl, tsl], reads=[R_gl[d][t]], writes=[r_gl])
                    for k in range(KC):
                        P.add("pe", lambda e, pa=pa, k=k, dsl=dsl, ATv=ATv: e.matmul(PS[pa][:], lhsT=WAO[:, k, dsl], rhs=ATv[:, k, :],
                                                                                 start=(k == 0), stop=(k == 7)), [r_wao, r_att], [RPS[pa]])
                    for k in range(KC):
                        P.add("pe", lambda e, pl=pl, k=k, dsl=dsl, LRv=LRv: e.matmul(PS[pl][:], lhsT=WLO[:, k, dsl], rhs=LRv[:, k, :],
                                                                                 start=(k == 0), stop=(k == 7)), [r_wlo, r_lrt], [RPS[pl]])
                    m1, r_m1 = M1[d % 2]
                    m2, r_m2 = M2[d % 2]
                    P.add("dve", lambda e, m1=m1, pa=pa, ga=ga: e.tensor_tensor(out=m1, in0=PS[pa][:], in1=ga, op=ALU.mult), [RPS[pa], r_ga], [r_m1])
                    P.add("dve", lambda e, m2=m2, pl=pl, gl=gl: e.tensor_tensor(out=m2, in0=PS[pl][:], in1=gl, op=ALU.mult), [RPS[pl], r_gl], [r_m2])
                    P.add("pool", lambda e, m1=m1, m2=m2, d=d: e.tensor_tensor(out=MT[d][0], in0=m1, in1=m2, op=ALU.add), [r_m1, r_m2], [MT[d][1]])
                for d in range(KC):
                    pb = 4 + d % 2
                    dsl = slice(d * 128, (d + 1) * 128)
                    xo, r_xo = XO[xi % 3]
                    xi += 1
                    P.dma("sp", xo, xa[dsl, tsl], reads=[R_xa[t][d]], writes=[r_xo])
                    for k in range(KC):
                        P.add("pe", lambda e, pb=pb, k=k, dsl=dsl: e.matmul(PS[pb][:], lhsT=WO[:, k, dsl], rhs=MT[k][0], start=(k == 0), stop=(k == 7)),
                              [r_wo, MT[k][1]], [RPS[pb]])
                    P.add("dve", lambda e, pb=pb, d=d, xo=xo: e.scalar_tensor_tensor(out=xo, in0=PS[pb][:], scalar=dvc(DV_G + l * 24 + 8, d),
                                                                                 in1=xo, op0=ALU.mult, op1=ALU.add), [RPS[pb], R_dv, r_xo], [r_xo])
                    P.dma("sp", xa[dsl, tsl], xo, reads=[r_xo], writes=[R_xa[t][d]])

        def final_phase():
            o = 90112
            XN = []
            for i in range(2):
                XN.append(P.buf(o, F32, [KC, T], f"FXN{i}")); o += 16384
            SQ = []
            for i in range(2):
                SQ.append(P.buf(o, BF16, [T], "FSQ")); o += 1024
            RSTD, r_rstd = P.buf(o, F32, [T], "FRSTD"); o += 2048
            fin = []
            for t in range(NT):
                XNv, r_xn = XN[t % 2]
                P.dma("sp", XNv, xview(xa, t), reads=R_xa[t], writes=[r_xn])
                for k in range(KC):
                    sq, r_sq = SQ[k % 2]
                    P.add("act", lambda e, k=k, sq=sq, XNv=XNv: e.activation(out=sq, in_=XNv[:, k, :], func=AF.Square), [r_xn], [r_sq])
                    P.add("pe", lambda e, k=k, sq=sq: e.matmul(PS[6][:], lhsT=ones[:], rhs=sq, start=(k == 0), stop=(k == 7)),
                          [R_ones, r_sq], [RPS[6]])
                P.add("act", lambda e: e.activation(out=RSTD, in_=PS[6][:], func=AF.Ln, scale=1.0 / D, bias=EPS), [RPS[6]], [r_rstd])
                P.add("act", lambda e: e.activation(out=RSTD, in_=RSTD, func=AF.Exp, scale=-0.5), [r_rstd], [r_rstd])
                for k in range(KC):
                    P.add("dve", lambda e, k=k, XNv=XNv: e.scalar_tensor_tensor(out=XNv[:, k, :], in0=XNv[:, k, :], scalar=vcol("fg", k),
                                                                           in1=RSTD, op0=ALU.mult, op1=ALU.mult), [r_xn, r_rstd, R_vec], [r_xn])
                fin.append(P.dma("sp", xview(outT, t), XNv, reads=[r_xn]))
            return fin

        def copy_out():
            fin = []
            for t in range(NT):
                fin.append(P.dma("sp", outT[:, t * T:(t + 1) * T], xa[:, t * T:(t + 1) * T], reads=R_xa[t]))
            return fin

        phases = []
        for l in range(n_layers):
            phases += [("ffn", l, 0), ("m1", l), ("m23", l), ("m3b", l), ("ffn", l, 1)]
        mod_phase()
        fin = None
        for i, ph in enumerate(phases):
            if ph[0] == "ffn":
                ffn_phase(ph[1], ph[2], xT_in if i == 0 else xa)
            elif ph[0] == "m1":
                m1_phase(ph[1])
            elif ph[0] == "m2":
                m2_phase(ph[1])
            elif ph[0] == "m3":
                m3_phase(ph[1])
            elif ph[0] == "m23":
                m23_phase(ph[1])
            elif ph[0] == "m3b":
                m3b_phase(ph[1])
            if stop_after is not None and i == stop_after:
                fin = copy_out()
                break
        if fin is None:
            fin = final_phase()
        P.emit(fin)
        STATS.clear()
        STATS.update({e: len(P.ops[e]) for e in ENGS})
        STATS["sems"] = P.n_sems
    return nc


def make_in_maps(inputs):
    inp = {k: np.asarray(v) for k, v in inputs.items()}
    cos2, sin2 = _rope_tables()
    w_in = np.array(inp["w_in"], np.float32, copy=True)
    for h in range(NH + NKV):
        w_in[:, :, h * 128:(h + 1) * 128] = inp["w_in"][:, :, h * 128 + ROPE_PERM]
    shared = {
        "cos2": cos2, "sin2": sin2,
        "ada_w": np.ascontiguousarray(inp["ada_w"], np.float32),
        "ffn1_up": np.ascontiguousarray(inp["ffn1_up"], np.float32),
        "ffn2_up": np.ascontiguousarray(inp["ffn2_up"], np.float32),
        "ffn1_down": np.ascontiguousarray(inp["ffn1_down"], np.float32),
        "ffn2_down": np.ascontiguousarray(inp["ffn2_down"], np.float32),
        "w_in": w_in,
        "lru_wa": np.ascontiguousarray(inp["lru_wa"], np.float32),
        "lru_wx": np.ascontiguousarray(inp["lru_wx"], np.float32),
        "w_attn_o": np.ascontiguousarray(inp["w_attn_o"], np.float32),
        "w_lru_o": np.ascontiguousarray(inp["w_lru_o"], np.float32),
        "w_out": np.ascontiguousarray(inp["w_out"], np.float32),
    }
    maps = []
    for b in range(inp["x"].shape[0]):
        m = dict(shared)
        m["xT"] = np.ascontiguousarray(inp["x"][b].T, np.float32)
        m["vec"] = _pack_vec(inp, b)
        maps.append(m)
    return maps


_NC_CACHE = {}


def kernel(**inputs):
    if "nc" not in _NC_CACHE:
        _NC_CACHE["nc"] = build_nc()
    nc = _NC_CACHE["nc"]
    in_maps = make_in_maps(inputs)
    res = run_bass_kernel_spmd(nc, in_maps, core_ids=list(range(8)))
    out = np.stack([np.ascontiguousarray(np.asarray(r["outT"]).T) for r in res.results], 0)
    return out.astype(np.float32)
```
